# Optimizing a Trainium2 kernel written in Bass

```python
import jax, jax.numpy as jnp
from jax import lax
import numpy as np

D_MODEL = 2048
BATCH = 16
SEQ = 2048
DEPTH = 4
DEC_BATCH = 8
DEC_SEQ = 2048
PAST_LEN = 128

N_MIXERS = 2
N_HEADS = 16
N_KV_HEADS = 4
HEAD_DIM = D_MODEL // N_HEADS
Q_PER_KV = N_HEADS // N_KV_HEADS
ROPE_THETA = 10000.0
GRID_W = 64
Q_BLOCK = 128
N_META = 16
POOL_WINDOWS = (2, 4, 8, 16)
N_POOL_GROUPS = len(POOL_WINDOWS)
POOL_GROUP_DIM = D_MODEL // N_POOL_GROUPS
D_FF = 4 * D_MODEL
N_ATTN_LAYERS = (DEPTH + 1) // 2
N_POOL_LAYERS = DEPTH // 2
EPS = 1e-6

kernel_name = "hybrid_attn_pool_encoder"


def rmsnorm(x, gain):
    xf = x.astype(jnp.float32)
    xf = xf * lax.rsqrt(jnp.mean(xf * xf, axis=-1, keepdims=True) + EPS)
    return (xf * gain.astype(jnp.float32)).astype(x.dtype)


def head_rmsnorm_f32(x, gain):
    xf = x.astype(jnp.float32)
    xf = xf * lax.rsqrt(jnp.mean(xf * xf, axis=-1, keepdims=True) + EPS)
    return xf * gain.astype(jnp.float32)


def axial_rope_tables(n_tokens):
    rows = n_tokens // GRID_W
    t = np.arange(rows * GRID_W)
    r = (t // GRID_W).astype(np.float32)
    c = (t % GRID_W).astype(np.float32)
    axis_dim = HEAD_DIM // 2
    inv_freq = (ROPE_THETA ** (-np.arange(0, axis_dim, 2, dtype=np.float32) / axis_dim)).astype(np.float32)
    ang = np.concatenate([r[:, None] * inv_freq[None], c[:, None] * inv_freq[None]], axis=-1)
    ang = np.concatenate([np.zeros((N_META, HEAD_DIM // 2), np.float32), ang], axis=0)
    return jnp.asarray(np.cos(ang), jnp.float32), jnp.asarray(np.sin(ang), jnp.float32)


def apply_rope(x, cos, sin):
    B, L, H, _ = x.shape
    xp = x.reshape(B, L, H, HEAD_DIM // 2, 2)
    x0, x1 = xp[..., 0], xp[..., 1]
    c = cos[None, :, None, :]
    s = sin[None, :, None, :]
    out = jnp.stack([x0 * c - x1 * s, x0 * s + x1 * c], axis=-1)
    return out.reshape(B, L, H, HEAD_DIM)


def attention_mixer(h, w_qkv, q_gain, k_gain, w_o, cos, sin):
    B, L, _ = h.shape
    qkv = h @ w_qkv
    q, k, v = jnp.split(qkv, [N_HEADS * HEAD_DIM, (N_HEADS + N_KV_HEADS) * HEAD_DIM], axis=-1)
    q = q.reshape(B, L, N_HEADS, HEAD_DIM)
    k = k.reshape(B, L, N_KV_HEADS, HEAD_DIM)
    v = v.reshape(B, L, N_KV_HEADS, HEAD_DIM)
    q = (apply_rope(head_rmsnorm_f32(q, q_gain), cos, sin) * (HEAD_DIM ** -0.5)).astype(h.dtype)
    k = apply_rope(head_rmsnorm_f32(k, k_gain), cos, sin).astype(h.dtype)
    q = q.reshape(B, L, N_KV_HEADS, Q_PER_KV, HEAD_DIM)

    def attend(qb):
        s = jnp.einsum('bqkgd,bskd->bkgqs', qb, k, preferred_element_type=jnp.float32)
        p = jax.nn.softmax(s, axis=-1).astype(v.dtype)
        return jnp.einsum('bkgqs,bskd->bqkgd', p, v)

    o_meta = attend(q[:, :N_META])
    n_real = L - N_META
    n_blk = n_real // Q_BLOCK
    q_real = q[:, N_META:].reshape(B, n_blk, Q_BLOCK, N_KV_HEADS, Q_PER_KV, HEAD_DIM)
    q_real = q_real.transpose(1, 0, 2, 3, 4, 5)
    o_real = lax.map(attend, q_real).transpose(1, 0, 2, 3, 4, 5)
    o_real = o_real.reshape(B, n_real, N_KV_HEADS, Q_PER_KV, HEAD_DIM)
    o = jnp.concatenate([o_meta, o_real], axis=1).reshape(B, L, N_HEADS * HEAD_DIM)
    return o @ w_o


def pool_mixer(h, w_pool, pool_scale):
    B, L, _ = h.shape
    hg = h.reshape(B, L, N_POOL_GROUPS, POOL_GROUP_DIM)
    csum = jnp.cumsum(hg.astype(jnp.float32), axis=1)
    csum = jnp.concatenate([jnp.zeros((B, 1, N_POOL_GROUPS, POOL_GROUP_DIM), jnp.float32), csum], axis=1)
    t = np.arange(L)
    pooled = []
    for g, w in enumerate(POOL_WINDOWS):
        lo = np.clip(t - w // 2, 0, L)
        hi = np.clip(t - w // 2 + w, 0, L)
        cnt = (hi - lo).astype(np.float32)
        cg = csum[:, :, g]
        window_sum = jnp.take(cg, hi, axis=1) - jnp.take(cg, lo, axis=1)
        pooled.append(window_sum / cnt[None, :, None])
    pooled = jnp.stack(pooled, axis=2)
    mixed = (pooled - hg.astype(jnp.float32)).astype(h.dtype)
    y = jnp.einsum('blgc,gcd->blgd', mixed, w_pool).reshape(B, L, D_MODEL)
    return y * pool_scale


def sqrelu_mlp(h, w_up, w_down):
    return jnp.square(jax.nn.relu(h @ w_up)) @ w_down


def trunk(x, meta_tokens, attn_norm, w_qkv, q_norm, k_norm, w_o,
          pool_norm, w_pool, pool_scale, mlp_norm, w_up, w_down, final_norm):
    B, N, _ = x.shape
    cos, sin = axial_rope_tables(N)
    meta = jnp.broadcast_to(meta_tokens[None].astype(x.dtype), (B, N_META, D_MODEL))
    h = jnp.concatenate([meta, x], axis=1)
    for i in range(DEPTH):
        j = i // N_MIXERS
        if i % N_MIXERS == 0:
            h = h + attention_mixer(rmsnorm(h, attn_norm[j]), w_qkv[j], q_norm[j], k_norm[j], w_o[j], cos, sin)
        else:
            h = h + pool_mixer(rmsnorm(h, pool_norm[j]), w_pool[j], pool_scale[j])
        h = h + sqrelu_mlp(rmsnorm(h, mlp_norm[i]), w_up[i], w_down[i])
    return rmsnorm(h, final_norm)[:, N_META:]


def setup_inputs(seed: int = 0) -> dict:
    key = jax.random.key(seed)
    ks = jax.random.split(key, 16)
    f32 = jnp.float32
    qkv_out = (N_HEADS + 2 * N_KV_HEADS) * HEAD_DIM
    nrm = lambda k, shape, scale: jax.random.normal(k, shape, f32) * scale
    return {
        "x_prompt": nrm(ks[0], (BATCH, SEQ, D_MODEL), 1.0),
        "x_sample": nrm(ks[1], (DEC_BATCH, DEC_SEQ, D_MODEL), 1.0),
        "meta_tokens": nrm(ks[2], (N_META, D_MODEL), 1.0),
        "attn_norm": 1.0 + nrm(ks[3], (N_ATTN_LAYERS, D_MODEL), 0.02),
        "w_qkv": nrm(ks[4], (N_ATTN_LAYERS, D_MODEL, qkv_out), D_MODEL ** -0.5),
        "q_norm": 1.0 + nrm(ks[5], (N_ATTN_LAYERS, HEAD_DIM), 0.02),
        "k_norm": 1.0 + nrm(ks[6], (N_ATTN_LAYERS, HEAD_DIM), 0.02),
        "w_o": nrm(ks[7], (N_ATTN_LAYERS, N_HEADS * HEAD_DIM, D_MODEL), (N_HEADS * HEAD_DIM) ** -0.5),
        "pool_norm": 1.0 + nrm(ks[8], (N_POOL_LAYERS, D_MODEL), 0.02),
        "w_pool": nrm(ks[9], (N_POOL_LAYERS, N_POOL_GROUPS, POOL_GROUP_DIM, POOL_GROUP_DIM), POOL_GROUP_DIM ** -0.5),
        "pool_scale": 1.0 + nrm(ks[10], (N_POOL_LAYERS, D_MODEL), 0.02),
        "mlp_norm": 1.0 + nrm(ks[11], (DEPTH, D_MODEL), 0.02),
        "w_up": nrm(ks[12], (DEPTH, D_MODEL, D_FF), D_MODEL ** -0.5),
        "w_down": nrm(ks[13], (DEPTH, D_FF, D_MODEL), D_FF ** -0.5),
        "final_norm": 1.0 + nrm(ks[14], (D_MODEL,), 0.02),
    }


def reference(x_prompt, x_sample, meta_tokens, attn_norm, w_qkv, q_norm, k_norm, w_o,
              pool_norm, w_pool, pool_scale, mlp_norm, w_up, w_down, final_norm):
    y_prompt = trunk(x_prompt, meta_tokens, attn_norm, w_qkv, q_norm, k_norm, w_o,
                     pool_norm, w_pool, pool_scale, mlp_norm, w_up, w_down, final_norm)
    y_sample = trunk(x_sample, meta_tokens, attn_norm, w_qkv, q_norm, k_norm, w_o,
                     pool_norm, w_pool, pool_scale, mlp_norm, w_up, w_down, final_norm)
    return (y_prompt, y_sample)
```

```python
import os
from contextlib import ExitStack
import numpy as np
import concourse.bass as bass
import concourse.mybir as mybir
from concourse.bass_utils import run_bass_kernel_spmd

F32 = mybir.dt.float32
BF16 = mybir.dt.bfloat16
ALU = mybir.AluOpType
AF = mybir.ActivationFunctionType

D = 2048
DC = 16
FF = 8192
FC = 64
NH = 16
NKV = 4
HD = 128
NMETA = 16
SEQ = 2048
L = SEQ + NMETA
GRID_W = 64
EPS = 1e-6
NCORES = 8
POOL_W = (2, 4, 8, 16)
HALO = 8

V_ATTN = 0
V_POOLN = 32
V_POOLS = 64
V_MLP = 96
V_FINAL = 160
V_QG = 176
V_KG = 178
NV = 180


def _tiles(n_tiles=5):
    base = L // n_tiles
    rem = L - base * n_tiles
    out = []
    a = 0
    for i in range(n_tiles):
        t = base + (1 if i < rem else 0)
        out.append((a, t))
        a += t
    return out


TILES = _tiles(5)
PTILES = _tiles(8)
PTM = max(t for _, t in PTILES)
TM = max(t for _, t in TILES)
KVW = 256
KV_TILES = [(i * KVW, KVW) for i in range(SEQ // KVW)] + [(2048, 16)]
KCHUNKS = [(i * 128, 128) for i in range(16)] + [(2048, 16)]


class Ev:
    __slots__ = ("key", "sem", "val", "eng")

    def __init__(self, key, sem, val, eng):
        self.key = key
        self.sem = sem
        self.val = val
        self.eng = eng


class Tracker:
    def __init__(self, nc, es, n_dma_sems=20):
        self.nc = nc
        self.eng = {"pe": nc.tensor, "act": nc.scalar, "dve": nc.vector, "pool": nc.gpsimd, "sp": nc.sync}
        self.csem = {k: es.enter_context(nc.semaphore("s_" + k)) for k in ("pe", "act", "dve", "pool")}
        self.cnt = {k: 0 for k in self.csem}
        self.seen = {k: {} for k in self.eng}
        self.dsem = {}
        self.dnext = {}
        for q in ("sp", "pool"):
            self.dsem[q] = [es.enter_context(nc.semaphore("d_%s%d" % (q, i))) for i in range(n_dma_sems)]
            self.dnext[q] = 0
        self.dcnt = {}
        self.tok = {}

    def _st(self, t):
        s = self.tok.get(t)
        if s is None:
            s = [{}, {}]
            self.tok[t] = s
        return s

    def _wait(self, e, ev):
        if self.seen[e].get(ev.key, 0) >= ev.val:
            return
        self.eng[e].wait_ge(ev.sem, ev.val)
        self.seen[e][ev.key] = ev.val

    def _deps(self, e, reads, writes, acc):
        for r in reads:
            for ev in self._st(r)[0].values():
                if ev.eng == e:
                    if e != "pe":
                        self._wait(e, ev)
                else:
                    self._wait(e, ev)
        for w in writes:
            st = self._st(w)
            evs = list(st[1].values())
            if not acc:
                evs += list(st[0].values())
            for ev in evs:
                if ev.eng == e:
                    continue
                self._wait(e, ev)

    def _commit(self, ev, reads, writes, acc):
        for w in writes:
            st = self._st(w)
            if acc and not st[1]:
                st[0][ev.key] = ev
            else:
                st[0] = {ev.key: ev}
                st[1] = {}
        for r in reads:
            if r in writes:
                continue
            self._st(r)[1][ev.key] = ev

    def op(self, e, fn, reads=(), writes=()):
        self._deps(e, reads, writes, False)
        ins = fn(self.eng[e])
        self.cnt[e] += 1
        ins.then_inc(self.csem[e], 1)
        ev = Ev(e, self.csem[e], self.cnt[e], e)
        self._commit(ev, reads, writes, False)
        return ev

    def dma(self, q, out, in_, reads=(), writes=(), acc=False, **kw):
        self._deps(q, reads, writes, acc)
        i = self.dnext[q]
        self.dnext[q] = (i + 1) % len(self.dsem[q])
        key = "d_%s%d" % (q, i)
        sem = self.dsem[q][i]
        prev = self.dcnt.get(key, 0)
        if prev:
            self._wait(q, Ev(key, sem, prev, None))
        ins = self.eng[q].dma_start(out=out, in_=in_, **kw)
        ins.then_inc(sem, 16)
        self.dcnt[key] = prev + 16
        ev = Ev(key, sem, prev + 16, None)
        self._commit(ev, reads, writes, acc)
        return ev

    def barrier(self, engines=("pe", "act", "dve", "pool", "sp")):
        evs = [Ev(k, self.csem[k], self.cnt[k], k) for k in self.csem if self.cnt[k] > 0]
        for q in ("sp",):
            for i, sem in enumerate(self.dsem[q]):
                key = "d_%s%d" % (q, i)
                if self.dcnt.get(key, 0):
                    evs.append(Ev(key, sem, self.dcnt[key], None))
        for e in engines:
            for ev in evs:
                if ev.eng == e:
                    continue
                self._wait(e, ev)
        self.tok = {k: v for k, v in self.tok.items() if isinstance(k, tuple) and k[0] == "wbf"}


class Prog:
    def __init__(self, nseq, nlayers, nslots=3):
        self.nseq = nseq
        self.nlayers = nlayers
        self.nslots = nslots
        self.uid = 0
        self.tiling = {}

    def sb(self, es, shape, dtype, name="t"):
        self.uid += 1
        return es.enter_context(self.nc.sbuf_tensor("%s_%d" % (name, self.uid), list(shape), dtype))

    def nb(self):
        b = self.bank_rr[self.bank_i % len(self.bank_rr)]
        self.bank_i += 1
        return b

    def plan_weights(self):
        sched = []
        for s in range(self.nseq):
            for i in range(self.nlayers):
                j = i // 2
                if i % 2 == 0:
                    for _ in KV_TILES:
                        sched.append(("qkv", j, 4))
                        sched.append(("qkv", j, 5))
                    for _ in TILES:
                        for b in range(4):
                            sched.append(("qkv", j, b))
                        for b in range(4):
                            sched.append(("wo", j, b))
                else:
                    for _ in PTILES:
                        for g in range(4):
                            sched.append(("pool", j, g))
                for _ in TILES:
                    for fb in range(16):
                        sched.append(("up", i, fb))
                    for jq in range(4):
                        for fb4 in range(4):
                            sched.append(("down", i, jq, fb4))
        self.sched = sched
        self.w_next_load = 0
        self.w_cur = -1

    def _w_src(self, key):
        kind = key[0]
        if kind == "qkv":
            _, j, b = key
            return (self.wb["qkv"][j][:, b * 512:(b + 1) * 512].rearrange("(kc p) n -> p kc n", p=128), 16,
                    ("wbf", "qkv", j))
        if kind == "wo":
            _, j, b = key
            return (self.wb["wo"][j][:, b * 512:(b + 1) * 512].rearrange("(kc p) n -> p kc n", p=128), 16,
                    ("wbf", "wo", j))
        if kind == "pool":
            _, j, g = key
            return (self.wb["pool"][j][g * 512:(g + 1) * 512, :].rearrange("(kc p) n -> p kc n", p=128), 4,
                    ("wbf", "pool", j))
        if kind == "up":
            _, i, fb = key
            return (self.wb["up"][i][:, fb * 512:(fb + 1) * 512].rearrange("(kc p) n -> p kc n", p=128), 16,
                    ("wbf", "up", i))
        _, i, jq, fb4 = key
        return (self.wb["down"][i][fb4 * 2048:(fb4 + 1) * 2048, jq * 512:(jq + 1) * 512]
                .rearrange("(kc p) n -> p kc n", p=128), 16, ("wbf", "down", i))

    def _w_emit_load(self):
        i = self.w_next_load
        if i >= len(self.sched):
            return
        src, nk, tok = self._w_src(self.sched[i])
        sl = i % self.nslots
        self.T.dma("sp", out=self.wslot[sl][:, 0:nk, :], in_=src, reads=[tok], writes=[("ws", sl)])
        self.w_next_load += 1

    def w_get(self, key):
        self.w_cur += 1
        assert self.sched[self.w_cur] == key, (self.sched[self.w_cur], key)
        while self.w_next_load <= self.w_cur:
            self._w_emit_load()
        sl = self.w_cur % self.nslots
        return self.wslot[sl], ("ws", sl)

    def w_done(self):
        while self.w_next_load < min(len(self.sched), self.w_cur + self.nslots + 1):
            self._w_emit_load()

    def cast_w(self, kind, j):
        src = self.w32[kind][j]
        dst = self.wb[kind][j]
        rows, cols = src.shape
        step = max(1, (4 * 1024 * 1024) // cols)
        for r0 in range(0, rows, step):
            r1 = min(rows, r0 + step)
            self.T.dma("pool", out=dst[r0:r1, :], in_=src[r0:r1, :], writes=[("wbf", kind, j)], acc=True,
                       max_dma_last_dim=2048)

    def lazy_casts(self, s, phase, j):
        if s != 0:
            return
        nl = self.nlayers
        todo = []
        if phase == "q":
            i = 2 * j
            todo += [("up", i), ("down", i)]
            if i + 1 < nl:
                todo += [("pool", j), ("up", i + 1), ("down", i + 1)]
        elif phase == "pool":
            if 2 * j + 2 < nl:
                todo += [("qkv", j + 1), ("wo", j + 1)]
        for kind, idx in todo:
            self.cast_w(kind, idx)

    def build(self):
        nc = bass.Bass("TRN2", target_bir_lowering=False)
        self.nc = nc
        nseq = self.nseq
        dt = nc.dram_tensor
        self.h0 = dt("h0", [nseq, D, L], F32, kind="ExternalInput").ap()
        self.yT = dt("yT", [nseq, D, SEQ], F32, kind="ExternalOutput").ap()
        w32 = {
            "qkv": dt("w_qkv", [2, D, 3072], F32, kind="ExternalInput").ap(),
            "wo": dt("w_o", [2, D, D], F32, kind="ExternalInput").ap(),
            "pool": dt("w_pool", [2, 2048, 512], F32, kind="ExternalInput").ap(),
            "up": dt("w_up", [4, D, FF], F32, kind="ExternalInput").ap(),
            "down": dt("w_down", [4, FF, D], F32, kind="ExternalInput").ap(),
        }
        self.wb = {
            "qkv": dt("wb_qkv", [2, D, 3072], BF16, kind="Internal").ap(),
            "wo": dt("wb_o", [2, D, D], BF16, kind="Internal").ap(),
            "pool": dt("wb_pool", [2, 2048, 512], BF16, kind="Internal").ap(),
            "up": dt("wb_up", [4, D, FF], BF16, kind="Internal").ap(),
            "down": dt("wb_down", [4, FF, D], BF16, kind="Internal").ap(),
        }
        vecs_d = dt("vecs", [128, NV], F32, kind="ExternalInput").ap()
        grep_d = dt("grep", [4, 128, 128], F32, kind="ExternalInput").ap()
        rot_d = dt("rot", [128, 128], F32, kind="ExternalInput").ap()
        cos_d = dt("cosT", [128, L], F32, kind="ExternalInput").ap()
        sin_d = dt("sinS", [128, L], F32, kind="ExternalInput").ap()
        self.cos_d, self.sin_d = cos_d, sin_d
        self.icnt_d = dt("icnt", [4, 128, L], F32, kind="ExternalInput").ap()
        self.hbuf = [dt("hA", [nseq, D, L], F32, kind="Internal").ap(),
                     dt("hB", [nseq, D, L], F32, kind="Internal").ap()]

        with ExitStack() as es:
            T = Tracker(nc, es)
            self.T = T
            self.pst = es.enter_context(nc.psum_tensor("pst", [128, 8, 512], F32))
            self.ps = [self.pst[:, i, :] for i in range(8)]
            self.bank_rr = list(range(8))
            self.bank_i = 0
            self.wslot = [self.sb(es, [128, 16, 512], BF16, "ws") for _ in range(self.nslots)]
            self.vecs = self.sb(es, [128, NV], F32, "vecs")
            self.rot = self.sb(es, [128, 128], F32, "rot")
            self.onesD = self.sb(es, [128, 128], BF16, "onesD")
            self.onesH = self.sb(es, [128, 128], BF16, "onesH")
            self.ones1 = self.sb(es, [128, 128], BF16, "ones1")
            self.epst = self.sb(es, [128, 1], F32, "eps")
            self.negb = self.sb(es, [128, 2], F32, "negb")
            gtmp = self.sb(es, [128, 4, 128], F32, "gtmp")
            gmax = self.sb(es, [128, 4], F32, "gmax")

            self.w32 = w32
            self.cast_w("qkv", 0)
            self.cast_w("wo", 0)
            T.dma("sp", out=self.vecs[:], in_=vecs_d[:, :], writes=["vecs"])
            T.dma("sp", out=self.rot[:], in_=rot_d[:, :], writes=["rot"])
            T.dma("sp", out=gtmp[:], in_=grep_d.rearrange("g p n -> p g n"), writes=["gtmp"])
            T.op("dve", lambda e: e.memset(self.onesD[:], 1.0 / D), writes=["onesD"])
            T.op("dve", lambda e: e.memset(self.onesH[:], 1.0 / HD), writes=["onesH"])
            T.op("dve", lambda e: e.memset(self.ones1[:], 1.0), writes=["ones1"])
            T.op("dve", lambda e: e.memset(self.epst[:], EPS), writes=["eps"])
            T.op("dve", lambda e: e.tensor_reduce(out=gmax[:], in_=gtmp[:], axis=mybir.AxisListType.X, op=ALU.max, apply_absolute_value=True),
                 reads=["gtmp"], writes=["gmax"])
            for j in range(2):
                T.op("dve", lambda e, j=j: e.scalar_tensor_tensor(
                    out=self.negb[:, j:j + 1], in0=gmax[:, j:j + 1], scalar=-float(np.sqrt(HD)),
                    in1=gmax[:, 2 + j:3 + j], op0=ALU.mult, op1=ALU.mult), reads=["gmax"], writes=["negb"])
            T.barrier()

            self.plan_weights()
            for s in range(nseq):
                sl = 0
                for i in range(self.nlayers):
                    j = i // 2
                    src = self.h0 if sl == 0 else self.hbuf[(sl - 1) % 2]
                    dst = self.hbuf[sl % 2]
                    stag = ("h0",) if sl == 0 else ("hb", (sl - 1) % 2)
                    dtag = ("hb", sl % 2)
                    if i % 2 == 0:
                        self.attn_layer(s, j, src, dst, stag, dtag)
                    else:
                        self.pool_layer(s, j, src, dst, stag, dtag)
                    sl += 1
                    src = self.hbuf[(sl - 1) % 2]
                    dst = self.hbuf[sl % 2]
                    stag = ("hb", (sl - 1) % 2)
                    dtag = ("hb", sl % 2)
                    self.mlp_layer(s, i, src, dst, stag, dtag, final=(i == self.nlayers - 1))
                    sl += 1
            T.barrier()
        return nc

    def load_tile(self, src, s, a, n, hb, htok, stag, col0=0, ngroups=1):
        T = self.T
        wt = self.tiling.get(stag + (s,), TILES)
        rd = [stag + (s, ti) for ti, (ta, tt) in enumerate(wt) if ta < a + n and ta + tt > a]
        gs = DC // ngroups
        for g in range(ngroups):
            T.dma("sp", out=hb[:, g * gs:(g + 1) * gs, col0:col0 + n],
                  in_=src[s][g * gs * 128:(g + 1) * gs * 128, a:a + n].rearrange("(c p) t -> p c t", p=128),
                  reads=rd, writes=[(htok, c) for c in range(g * gs, (g + 1) * gs)])

    def rmsnorm_a(self, hb, htok, n, sq, col0=0, sqtoks=("sq",)):
        hr = [(htok, c) for c in range(DC)]
        self.T.op("act", lambda e: e.activation(out=sq[:, :, 0:n], in_=hb[:, :, col0:col0 + n], func=AF.Square),
                  reads=hr, writes=list(sqtoks))

    def rmsnorm_b(self, hb, htok, n, gcol, sq, rstd, out, otok, col0=0, sqtoks=("sq",), engs=("dve",), rtok="rstd"):
        T = self.T
        b = self.nb()
        ps = self.ps[b]

        def mm(e):
            for c in range(DC):
                ins = e.matmul(ps[:, 0:n], lhsT=self.onesD[:, :], rhs=sq[:, c, 0:n], start=(c == 0), stop=(c == DC - 1))
            return ins
        T.op("pe", mm, reads=list(sqtoks) + ["onesD"], writes=[("ps", b)])
        T.op("act", lambda e: e.activation(out=rstd[:, 0:n], in_=ps[:, 0:n], func=AF.Ln, bias=self.epst[:, 0:1], scale=1.0),
             reads=[("ps", b), "eps"], writes=[rtok])
        T.op("act", lambda e: e.activation(out=rstd[:, 0:n], in_=rstd[:, 0:n], func=AF.Exp, scale=-0.5), reads=[rtok], writes=[rtok])
        for c in range(DC):
            T.op(engs[c % len(engs)], lambda e, c=c: e.scalar_tensor_tensor(
                out=out[:, c, 0:n], in0=hb[:, c, col0:col0 + n], scalar=self.vecs[:, gcol + c:gcol + c + 1],
                in1=rstd[:, 0:n], op0=ALU.mult, op1=ALU.mult),
                reads=[(htok, c), rtok, "vecs"], writes=[(otok, c)])

    def rmsnorm(self, hb, htok, n, gcol, sq, rstd, out, otok, col0=0, sqtoks=("sq",), engs=("dve",), rtok="rstd"):
        self.rmsnorm_a(hb, htok, n, sq, col0, sqtoks)
        self.rmsnorm_b(hb, htok, n, gcol, sq, rstd, out, otok, col0, sqtoks, engs, rtok)

    def store_tile(self, dst, s, ti, a, n, hb, htok, dtag, col0=0, tiles=None, ngroups=1):
        self.tiling[dtag + (s,)] = TILES if tiles is None else tiles
        gs = DC // ngroups
        for g in range(ngroups):
            self.T.dma("sp", out=dst[s][g * gs * 128:(g + 1) * gs * 128, a:a + n].rearrange("(c p) t -> p c t", p=128),
                       in_=hb[:, g * gs:(g + 1) * gs, col0:col0 + n],
                       reads=[(htok, c) for c in range(g * gs, (g + 1) * gs)], writes=[dtag + (s, ti)], acc=(ngroups > 1))

    def store_tile_group(self, dst, s, ti, a, n, hb, htok, dtag, g, gs=4):
        self.tiling[dtag + (s,)] = TILES
        self.T.dma("sp", out=dst[s][g * gs * 128:(g + 1) * gs * 128, a:a + n].rearrange("(c p) t -> p c t", p=128),
                   in_=hb[:, g * gs:(g + 1) * gs, 0:n],
                   reads=[(htok, c) for c in range(g * gs, (g + 1) * gs)], writes=[dtag + (s, ti)], acc=True)

    def mlp_layer(self, s, i, src, dst, stag, dtag, final):
        T = self.T
        with ExitStack() as es:
            hT = [self.sb(es, [128, DC, TM], F32, "hT") for _ in range(2)]
            sq = self.sb(es, [128, DC, TM], BF16, "sq")
            hn = self.sb(es, [128, DC, TM], BF16, "hn")
            rstd = self.sb(es, [128, TM], F32, "rstd")
            rl = [self.sb(es, [128, TM], F32, "rl") for _ in range(3)]
            uT = self.sb(es, [128, FC, TM], BF16, "uT")
            if final:
                sq2 = uT[:, 32:48, :]
                sq2toks = [("uT", f) for f in range(32, 48)]
                rstd2 = self.sb(es, [128, TM], F32, "rstd2")
            def prep_a(ti):
                a, n = TILES[ti]
                self.load_tile(src, s, a, n, hT[ti % 2], "hT%d" % (ti % 2), stag)
                self.rmsnorm_a(hT[ti % 2], "hT%d" % (ti % 2), n, sq)

            def prep_b(ti):
                a, n = TILES[ti]
                self.rmsnorm_b(hT[ti % 2], "hT%d" % (ti % 2), n, V_MLP + 16 * i, sq, rstd, hn, "hn")

            prep_a(0)
            prep_b(0)
            for ti, (a, n) in enumerate(TILES):
                hb = hT[ti % 2]
                htok = "hT%d" % (ti % 2)
                hnr = [("hn", c) for c in range(DC)]
                for fb in range(16):
                    slot, stok = self.w_get(("up", i, fb))
                    for fc in range(4):
                        f = fb * 4 + fc
                        b = self.nb()
                        ps = self.ps[b]

                        def mm(e, fc=fc, ps=ps, slot=slot):
                            for kc in range(DC):
                                ins = e.matmul(ps[:, 0:n], lhsT=slot[:, kc, fc * 128:(fc + 1) * 128], rhs=hn[:, kc, 0:n],
                                               start=(kc == 0), stop=(kc == DC - 1))
                            return ins
                        T.op("pe", mm, reads=[stok] + hnr, writes=[("ps", b)])
                        r = rl[f % 3]
                        T.op("act", lambda e, r=r, ps=ps: e.activation(out=r[:, 0:n], in_=ps[:, 0:n], func=AF.Relu),
                             reads=[("ps", b)], writes=[("rl", f % 3)])
                        T.op("dve", lambda e, r=r, f=f: e.tensor_tensor(out=uT[:, f, 0:n], in0=r[:, 0:n], in1=r[:, 0:n], op=ALU.mult),
                             reads=[("rl", f % 3)], writes=[("uT", f)])
                    self.w_done()
                if ti + 1 < len(TILES):
                    prep_a(ti + 1)
                for jq in range(4):
                    if jq == 1 and ti + 1 < len(TILES):
                        prep_b(ti + 1)
                    banks = [4 * (jq % 2) + dd for dd in range(4)]
                    for fb4 in range(4):
                        slot, stok = self.w_get(("down", i, jq, fb4))

                        def mm(e, slot=slot, fb4=fb4, banks=banks):
                            for fl in range(16):
                                f = fb4 * 16 + fl
                                for dd in range(4):
                                    ins = e.matmul(self.ps[banks[dd]][:, 0:n], lhsT=slot[:, fl, dd * 128:(dd + 1) * 128],
                                                   rhs=uT[:, f, 0:n], start=(f == 0), stop=(f == FC - 1))
                            return ins
                        T.op("pe", mm, reads=[stok] + [("uT", fb4 * 16 + fl) for fl in range(16)],
                             writes=[("ps", b) for b in banks])
                        self.w_done()
                    for dd in range(4):
                        c = jq * 4 + dd
                        b = banks[dd]
                        T.op("dve", lambda e, c=c, b=b: e.tensor_tensor(out=hb[:, c, 0:n], in0=self.ps[b][:, 0:n], in1=hb[:, c, 0:n], op=ALU.add),
                             reads=[("ps", b), (htok, c)], writes=[(htok, c)])
                if final:
                    self.rmsnorm(hb, htok, n, V_FINAL, sq2, rstd2, hb, htok, sqtoks=sq2toks, rtok="rstd2")
                    lo = max(a, NMETA)
                    T.dma("sp", out=self.yT[s][:, lo - NMETA:a + n - NMETA].rearrange("(c p) t -> p c t", p=128),
                          in_=hb[:, :, lo - a:n], reads=[(htok, c) for c in range(DC)], writes=[("y", s, ti)])
                else:
                    self.store_tile(dst, s, ti, a, n, hb, htok, dtag)
            T.barrier()

    def qk_pipeline(self, groups, n, a_off, cs, sn, sets, fillers=(), peng="pool", cstoks=("cs", "sn")):
        T = self.T
        G = len(groups)
        PA = [(0, 1), (2, 3)]
        SB = [(4, 5), (6, 7)]
        fillers = list(fillers)
        ns = len(sets)
        for it in range(G + 2):
            deferred = None
            g = it
            if g < G:
                pa = PA[g % 2]
                k = g % ns
                xraw, sqh, rsh, xn = sets[k]
                groups[g]["proj"](pa)
                pv = self.pst[:, pa[0]:pa[0] + 2, 0:n]
                prd = [("ps", pa[0]), ("ps", pa[1])]
                T.op("act", lambda e: e.copy(out=xraw[:, :, 0:n], in_=pv), reads=prd, writes=[("xraw", k)])
                T.op("act", lambda e: e.activation(out=sqh[:, :, 0:n], in_=pv, func=AF.Square), reads=prd, writes=[("sqh", k)])
            else:
                for _ in range(2):
                    if fillers:
                        fillers.pop(0)()
            g = it - 1
            if 0 <= g < G:
                sbk = SB[g % 2]
                k = g % ns
                xraw, sqh, rsh, xn = sets[k]
                gcol = groups[g]["gcol"]

                def st(e, sbk=sbk, sqh=sqh):
                    for i in range(2):
                        ins = e.matmul(self.ps[sbk[i]][:, 0:n], lhsT=self.onesH[:, :], rhs=sqh[:, i, 0:n], start=True, stop=True)
                    return ins
                T.op("pe", st, reads=[("sqh", k), "onesH"], writes=[("ps", sbk[0]), ("ps", sbk[1])])
                sv = self.pst[:, sbk[0]:sbk[0] + 2, 0:n]
                T.op("act", lambda e: e.activation(out=rsh[:, :, 0:n], in_=sv, func=AF.Ln, bias=self.epst[:, 0:1], scale=1.0),
                     reads=[("ps", sbk[0]), ("ps", sbk[1]), "eps"], writes=[("rsh", k)])
                T.op("act", lambda e: e.activation(out=rsh[:, :, 0:n], in_=rsh[:, :, 0:n], func=AF.Exp, scale=-0.5),
                     reads=[("rsh", k)], writes=[("rsh", k)])
                deferred = (xraw, rsh, xn, gcol, k)
            g = it - 2
            if 0 <= g < G:
                sbk = SB[g % 2]
                k = g % ns
                xraw, sqh, rsh, xn = sets[k]

                def rt(e, sbk=sbk, xn=xn):
                    for i in range(2):
                        ins = e.matmul(self.ps[sbk[i]][:, 0:n], lhsT=self.rot[:, :], rhs=xn[:, i, 0:n], start=True, stop=True)
                    return ins
                T.op("pe", rt, reads=[("xn", k), "rot"], writes=[("ps", sbk[0]), ("ps", sbk[1])])
                for i in range(2):
                    T.op(peng, lambda e, i=i: e.tensor_tensor(out=xraw[:, i, 0:n], in0=xn[:, i, 0:n], in1=cs[:, 0:n], op=ALU.mult),
                         reads=[("xn", k), cstoks[0]], writes=[("xraw", k)])
                for i in range(2):
                    T.op("dve", lambda e, i=i: e.tensor_tensor(out=rsh[:, i, 0:n], in0=self.ps[sbk[i]][:, 0:n], in1=sn[:, 0:n], op=ALU.mult),
                         reads=[("ps", sbk[i]), cstoks[1]], writes=[("rsh", k)])
                for i in range(2):
                    oap, otok = groups[g]["outs"][i]
                    T.op(peng, lambda e, i=i, oap=oap: e.tensor_tensor(out=oap, in0=xraw[:, i, 0:n], in1=rsh[:, i, 0:n], op=ALU.add),
                         reads=[("xraw", k), ("rsh", k)], writes=[otok])
            if deferred is not None:
                xraw, rsh, xn, gcol, k = deferred
                T.op("dve", lambda e: e.scalar_tensor_tensor(out=xn[:, :, 0:n], in0=xraw[:, :, 0:n], scalar=self.vecs[:, gcol:gcol + 1],
                                                             in1=rsh[:, :, 0:n], op0=ALU.mult, op1=ALU.mult),
                     reads=[("xraw", k), ("rsh", k), "vecs"], writes=[("xn", k)])
        while fillers:
            fillers.pop(0)()

    def attn_layer(self, s, j, src, dst, stag, dtag):
        T = self.T
        scale = float(HD ** -0.5)
        with ExitStack() as es:
            kT = self.sb(es, [128, NKV, L], BF16, "kT")
            vS = self.sb(es, [128, 17, 512], BF16, "vS")
            cs = self.sb(es, [128, 512], F32, "cs")
            sn = self.sb(es, [128, 512], F32, "sn")
            rstd_all = self.sb(es, [128, L], F32, "rstda")

            def load_cs(a, n):
                T.dma("sp", out=cs[:, 0:n], in_=self.cos_d[:, a:a + n], writes=["cs"])
                T.dma("sp", out=sn[:, 0:n], in_=self.sin_d[:, a:a + n], writes=["sn"])
            with ExitStack() as es2:
                hbs = [self.sb(es2, [128, DC, KVW], F32, "hTk") for _ in range(2)]
                sq = self.sb(es2, [128, DC, KVW], BF16, "sqk")
                hns = [self.sb(es2, [128, DC, KVW], BF16, "hnk") for _ in range(2)]
                css = [(self.sb(es2, [128, KVW], F32, "csk"), self.sb(es2, [128, KVW], F32, "snk")) for _ in range(2)]
                sets = [(self.sb(es2, [128, 2, KVW], F32, "xraw"), self.sb(es2, [128, 2, KVW], BF16, "sqh"),
                         self.sb(es2, [128, 2, KVW], F32, "rsh"), self.sb(es2, [128, 2, KVW], F32, "xn")) for _ in range(3)]
                peng = "dve"

                def kv_prep_a(ti):
                    a, n = KV_TILES[ti]
                    p = ti % 2
                    self.load_tile(src, s, a, n, hbs[p], "hT%d" % p, stag)
                    T.dma("sp", out=css[p][0][:, 0:n], in_=self.cos_d[:, a:a + n], writes=["cs%d" % p])
                    T.dma("sp", out=css[p][1][:, 0:n], in_=self.sin_d[:, a:a + n], writes=["sn%d" % p])
                    self.rmsnorm_a(hbs[p], "hT%d" % p, n, sq)

                def kv_prep_b(ti):
                    a, n = KV_TILES[ti]
                    p = ti % 2
                    self.rmsnorm_b(hbs[p], "hT%d" % p, n, V_ATTN + 16 * j, sq, rstd_all[:, a:a + n], hns[p], "hn%d" % p,
                                   rtok=("rstda", ti))

                kv_prep_a(0)
                kv_prep_b(0)
                for ti, (a, n) in enumerate(KV_TILES):
                    p = ti % 2
                    hn = hns[p]
                    hnr = [("hn%d" % p, c) for c in range(DC)]
                    wst = {}
                    if ti + 1 < len(KV_TILES):
                        kv_prep_a(ti + 1)

                    def kproj(pa, g, n=n, wst=wst, hnr=hnr, hn=hn):
                        if g == 0:
                            wst["k"] = self.w_get(("qkv", j, 4))
                        slot, stok = wst["k"]
                        for i in range(2):
                            kvh = 2 * g + i
                            b = pa[i]

                            def mm(e, kvh=kvh, b=b):
                                for kc in range(DC):
                                    ins = e.matmul(self.ps[b][:, 0:n], lhsT=slot[:, kc, kvh * 128:(kvh + 1) * 128], rhs=hn[:, kc, 0:n],
                                                   start=(kc == 0), stop=(kc == DC - 1))
                                return ins
                            T.op("pe", mm, reads=[stok] + hnr, writes=[("ps", b)])
                        if g == 1:
                            self.w_done()

                    groups = [dict(proj=(lambda pa, g=g: kproj(pa, g)), gcol=V_KG + j,
                                   outs=[(kT[:, 2 * g + i, a:a + n], ("kT", 2 * g + i, ti)) for i in range(2)]) for g in range(2)]
                    nm = (n + 127) // 128
                    fills = []
                    for m in range(nm):
                        def vfill(m=m, n=n, a=a, nm=nm, wst=wst, hnr=hnr, hn=hn):
                            if m == 0:
                                wst["v"] = self.w_get(("qkv", j, 5))
                            slot, stok = wst["v"]
                            ms = min(128, n - m * 128)
                            b = self.nb()

                            def mm(e):
                                for kc in range(DC):
                                    ins = e.matmul(self.ps[b][0:ms, 0:512], lhsT=hn[:, kc, m * 128:m * 128 + ms], rhs=slot[:, kc, :],
                                                   start=(kc == 0), stop=(kc == DC - 1))
                                return ins
                            T.op("pe", mm, reads=[stok] + hnr, writes=[("ps", b)])
                            kc_i = a // 128 + m
                            T.op("act", lambda e: e.copy(out=vS[0:ms, kc_i, :], in_=self.ps[b][0:ms, 0:512]),
                                 reads=[("ps", b)], writes=[("vS", kc_i)])
                            if m == nm - 1:
                                self.w_done()
                        fills.append(vfill)
                    if ti + 1 < len(KV_TILES):
                        fills.append(lambda ti=ti: kv_prep_b(ti + 1))
                    self.qk_pipeline(groups, n, a, css[p][0], css[p][1], sets, fills, peng=peng, cstoks=("cs%d" % p, "sn%d" % p))
                T.barrier()
            self.lazy_casts(s, "q", j)
            with ExitStack() as es2:
                hb = self.sb(es2, [128, DC, TM], F32, "hTq")
                hn = self.sb(es2, [128, DC, TM], BF16, "hnq")
                qT = self.sb(es2, [128, NH, TM], BF16, "qT")
                oT = self.sb(es2, [128, NH, TM], BF16, "oT")
                P = [self.sb(es2, [128, TM], BF16, "P") for _ in range(4)]
                rcp = [self.sb(es2, [128, TM], F32, "rcp") for _ in range(2)]
                sets = [(self.sb(es2, [128, 2, TM], F32, "xraw"), self.sb(es2, [128, 2, TM], BF16, "sqh"),
                         self.sb(es2, [128, 2, TM], F32, "rsh"), self.sb(es2, [128, 2, TM], F32, "xn")) for _ in range(3)]
                htok = "hTq"
                otoks = [("oT", h) for h in range(NH)]
                for ti, (a, n) in enumerate(TILES):
                    self.load_tile(src, s, a, n, hb, htok, stag, ngroups=4)
                    load_cs(a, n)
                    gcol = V_ATTN + 16 * j
                    for c in range(DC):
                        T.op("dve", lambda e, c=c: e.scalar_tensor_tensor(
                            out=hn[:, c, 0:n], in0=hb[:, c, 0:n], scalar=self.vecs[:, gcol + c:gcol + c + 1],
                            in1=rstd_all[:, a:a + n], op0=ALU.mult, op1=ALU.mult),
                            reads=[(htok, c), "vecs"], writes=[("hn", c)])
                    hnr = [("hn", c) for c in range(DC)]
                    wst = {}

                    def qproj(pa, g, n=n, wst=wst, hnr=hnr):
                        if g % 2 == 0:
                            wst["q"] = self.w_get(("qkv", j, g // 2))
                        slot, stok = wst["q"]
                        for i in range(2):
                            hh = (2 * g + i) % 4
                            b = pa[i]

                            def mm(e, hh=hh, b=b):
                                for kc in range(DC):
                                    ins = e.matmul(self.ps[b][:, 0:n], lhsT=slot[:, kc, hh * 128:(hh + 1) * 128], rhs=hn[:, kc, 0:n],
                                                   start=(kc == 0), stop=(kc == DC - 1))
                                return ins
                            T.op("pe", mm, reads=[stok] + hnr, writes=[("ps", b)])
                        if g % 2 == 1:
                            self.w_done()

                    groups = [dict(proj=(lambda pa, g=g: qproj(pa, g)), gcol=V_QG + j,
                                   outs=[(qT[:, 2 * g + i, 0:n], ("qT", 2 * g + i)) for i in range(2)]) for g in range(NH // 2)]
                    self.qk_pipeline(groups, n, a, cs, sn, sets, peng="dve")
                    steps = [(h, kc) for h in range(NH) for kc in range(len(KCHUNKS))]
                    sbank = {}

                    def S(idx):
                        h, kc = steps[idx]
                        kvh = h // 4
                        k0, ks = KCHUNKS[kc]
                        b = idx % 3
                        sbank[idx] = b
                        T.op("pe", lambda e: e.matmul(self.ps[b][0:ks, 0:n], lhsT=kT[:, kvh, k0:k0 + ks], rhs=qT[:, h, 0:n], start=True, stop=True),
                             reads=[("qT", h)], writes=[("ps", b)])

                    S(0)
                    S(1)
                    for idx, (h, kc) in enumerate(steps):
                        kvh = h // 4
                        k0, ks = KCHUNKS[kc]
                        b = sbank[idx]
                        pi = idx % 4
                        ob = 3 + (h % 2)
                        sb_ = 5 + (h % 2)
                        T.op("act", lambda e, b=b, pi=pi, ks=ks: e.activation(out=P[pi][0:ks, 0:n], in_=self.ps[b][0:ks, 0:n], func=AF.Exp,
                                                                             bias=self.negb[0:ks, j:j + 1], scale=scale),
                             reads=[("ps", b), "negb"], writes=[("P", pi)])
                        if idx + 2 < len(steps):
                            S(idx + 2)
                        last = (kc == len(KCHUNKS) - 1)

                        def pv(e, pi=pi, ks=ks, kc=kc, kvh=kvh, ob=ob, sb_=sb_, last=last):
                            e.matmul(self.ps[ob][:, 0:n], lhsT=vS[0:ks, kc, kvh * 128:(kvh + 1) * 128], rhs=P[pi][0:ks, 0:n],
                                     start=(kc == 0), stop=last)
                            return e.matmul(self.ps[sb_][:, 0:n], lhsT=self.ones1[0:ks, :], rhs=P[pi][0:ks, 0:n],
                                            start=(kc == 0), stop=last)
                        T.op("pe", pv, reads=[("P", pi), "ones1"], writes=[("ps", ob), ("ps", sb_)])
                        if last:
                            rc = rcp[h % 2]
                            T.op("dve", lambda e, rc=rc, sb_=sb_: e.reciprocal(out=rc[:, 0:n], in_=self.ps[sb_][:, 0:n]),
                                 reads=[("ps", sb_)], writes=[("rcp", h % 2)])
                            T.op("dve", lambda e, rc=rc, ob=ob, h=h: e.tensor_tensor(out=oT[:, h, 0:n], in0=self.ps[ob][:, 0:n], in1=rc[:, 0:n], op=ALU.mult),
                                 reads=[("ps", ob), ("rcp", h % 2)], writes=[("oT", h)])
                    for ob_ in range(4):
                        slot, stok = self.w_get(("wo", j, ob_))
                        for dd in range(4):
                            d = ob_ * 4 + dd
                            b = self.nb()

                            def mm(e, dd=dd, b=b, slot=slot):
                                for c in range(NH):
                                    ins = e.matmul(self.ps[b][:, 0:n], lhsT=slot[:, c, dd * 128:(dd + 1) * 128], rhs=oT[:, c, 0:n],
                                                   start=(c == 0), stop=(c == NH - 1))
                                return ins
                            T.op("pe", mm, reads=[stok] + otoks, writes=[("ps", b)])
                            T.op("dve", lambda e, d=d, b=b: e.tensor_tensor(out=hb[:, d, 0:n], in0=self.ps[b][:, 0:n], in1=hb[:, d, 0:n], op=ALU.add),
                                 reads=[("ps", b), (htok, d)], writes=[(htok, d)])
                        self.store_tile_group(dst, s, ti, a, n, hb, htok, dtag, ob_)
                        self.w_done()
                T.barrier()

    def pool_layer(self, s, j, src, dst, stag, dtag):
        T = self.T
        W = PTM + 2 * HALO
        U0 = HALO
        self.lazy_casts(s, "pool", j)
        with ExitStack() as es:
            hbs = [self.sb(es, [128, DC, W], F32, "hTp") for _ in range(2)]
            sq = self.sb(es, [128, DC, W], BF16, "sqp")
            xf = self.sb(es, [128, DC, W], F32, "xf")
            rstds = [self.sb(es, [128, W], F32, "rstdp") for _ in range(2)]
            A = self.sb(es, [128, 16, W], F32, "pA")
            B = self.sb(es, [128, 12, W], F32, "pB")
            ic = self.sb(es, [128, 4, 2 * HALO], F32, "ic")
            E = self.sb(es, [128, DC, HALO], F32, "pE")
            mx = self.sb(es, [128, DC, PTM], BF16, "mx")

            def geom(ti):
                a, n = PTILES[ti]
                lo = max(0, a - HALO)
                hi = min(L, a + n + HALO)
                return a, n, lo, hi - lo, lo - (a - HALO)

            def prep(ti):
                a, n, lo, nl, c0 = geom(ti)
                p = ti % 2
                hb, htok = hbs[p], "hT%d" % p
                self.load_tile(src, s, lo, nl, hb, htok, stag, col0=c0)
                self.rmsnorm_a(hb, htok, nl, sq, col0=c0)
                b = self.nb()
                ps = self.ps[b]

                def mm(e):
                    for c in range(DC):
                        ins = e.matmul(ps[:, 0:nl], lhsT=self.onesD[:, :], rhs=sq[:, c, 0:nl], start=(c == 0), stop=(c == DC - 1))
                    return ins
                T.op("pe", mm, reads=["sq", "onesD"], writes=[("ps", b)])
                rstd, rtok = rstds[p], "rstd%d" % p
                T.op("act", lambda e: e.activation(out=rstd[:, 0:nl], in_=ps[:, 0:nl], func=AF.Ln, bias=self.epst[:, 0:1], scale=1.0),
                     reads=[("ps", b), "eps"], writes=[rtok])
                T.op("act", lambda e: e.activation(out=rstd[:, 0:nl], in_=rstd[:, 0:nl], func=AF.Exp, scale=-0.5), reads=[rtok], writes=[rtok])

            prep(0)
            for ti in range(len(PTILES)):
                a, n, lo, nl, c0 = geom(ti)
                p = ti % 2
                hb, htok = hbs[p], "hT%d" % p
                rstd, rtok = rstds[p], "rstd%d" % p
                xtok = [("xf", c) for c in range(DC)]
                if c0 > 0:
                    T.op("dve", lambda e: e.memset(xf[:, :, 0:c0], 0.0), writes=xtok)
                if c0 + nl < n + 2 * HALO:
                    T.op("dve", lambda e: e.memset(xf[:, :, c0 + nl:n + 2 * HALO], 0.0), writes=xtok)
                gcol = V_POOLN + 16 * j
                for c in range(DC):
                    T.op("dve", lambda e, c=c: e.scalar_tensor_tensor(
                        out=xf[:, c, c0:c0 + nl], in0=hb[:, c, c0:c0 + nl], scalar=self.vecs[:, gcol + c:gcol + c + 1],
                        in1=rstd[:, 0:nl], op0=ALU.mult, op1=ALU.mult),
                        reads=[(htok, c), rtok, "vecs"], writes=[("xf", c)])
                if ti + 1 < len(PTILES):
                    prep(ti + 1)
                T.op("dve", lambda e: e.tensor_tensor(out=A[:, :, 1:W], in0=xf[:, :, 0:W - 1], in1=xf[:, :, 1:W], op=ALU.add),
                     reads=xtok, writes=["pA"])
                T.op("dve", lambda e: e.tensor_tensor(out=B[:, :, 2:W - 1], in0=A[:, 4:16, 1:W - 2], in1=A[:, 4:16, 3:W], op=ALU.add),
                     reads=["pA"], writes=["pB"])
                T.op("dve", lambda e: e.tensor_tensor(out=A[:, 8:16, 4:W - 3], in0=B[:, 4:12, 2:W - 5], in1=B[:, 4:12, 6:W - 1], op=ALU.add),
                     reads=["pB"], writes=["pA"])
                T.op("dve", lambda e: e.tensor_tensor(out=B[:, 8:12, 8:W - 7], in0=A[:, 12:16, 4:W - 11], in1=A[:, 12:16, 12:W - 3], op=ALU.add),
                     reads=["pA"], writes=["pB"])
                assert U0 + n <= W - 7
                srcs = [A[:, 0:4, :], B[:, 0:4, :], A[:, 8:12, :], B[:, 8:12, :]]
                stoks = ["pA", "pB", "pA", "pB"]
                for g, w in enumerate(POOL_W):
                    T.op("dve", lambda e, g=g, w=w: e.scalar_tensor_tensor(
                        out=mx[:, 4 * g:4 * g + 4, 0:n], in0=srcs[g][:, :, U0:U0 + n], scalar=1.0 / w,
                        in1=xf[:, 4 * g:4 * g + 4, U0:U0 + n], op0=ALU.mult, op1=ALU.subtract),
                        reads=[stoks[g]] + [("xf", 4 * g + c) for c in range(4)], writes=[("mx", 4 * g + c) for c in range(4)])
                edges = []
                if a == 0:
                    edges.append((0, 0))
                if a + n == L:
                    edges.append((n - HALO, 1))
                for (e0, ei) in edges:
                    T.dma("sp", out=ic[:, :, ei * HALO:(ei + 1) * HALO],
                          in_=self.icnt_d[:, :, a + e0:a + e0 + HALO].rearrange("g p t -> p g t"), writes=[("ic", ei)])
                    for c in range(DC):
                        g = c // 4
                        T.op("dve", lambda e, c=c, g=g, e0=e0, ei=ei: e.tensor_tensor(
                            out=E[:, c, :], in0=srcs[g][:, c % 4, U0 + e0:U0 + e0 + HALO], in1=ic[:, g, ei * HALO:(ei + 1) * HALO], op=ALU.mult),
                            reads=[stoks[g], ("ic", ei)], writes=[("pE", c)])
                    for c in range(DC):
                        T.op("dve", lambda e, c=c, e0=e0: e.tensor_tensor(
                            out=mx[:, c, e0:e0 + HALO], in0=E[:, c, :], in1=xf[:, c, U0 + e0:U0 + e0 + HALO], op=ALU.subtract),
                            reads=[("pE", c), ("xf", c)], writes=[("mx", c)])
                for g in range(4):
                    slot, stok = self.w_get(("pool", j, g))
                    for oc in range(4):
                        d = 4 * g + oc
                        b = self.nb()

                        def mm(e, oc=oc, b=b, slot=slot, g=g):
                            for icc in range(4):
                                ins = e.matmul(self.ps[b][:, 0:n], lhsT=slot[:, icc, oc * 128:(oc + 1) * 128], rhs=mx[:, 4 * g + icc, 0:n],
                                               start=(icc == 0), stop=(icc == 3))
                            return ins
                        T.op("pe", mm, reads=[stok] + [("mx", 4 * g + icc) for icc in range(4)], writes=[("ps", b)])
                        T.op("dve", lambda e, d=d, b=b: e.scalar_tensor_tensor(
                            out=hb[:, d, HALO:HALO + n], in0=self.ps[b][:, 0:n], scalar=self.vecs[:, V_POOLS + 16 * j + d:V_POOLS + 16 * j + d + 1],
                            in1=hb[:, d, HALO:HALO + n], op0=ALU.mult, op1=ALU.add),
                            reads=[("ps", b), (htok, d), "vecs"], writes=[(htok, d)])
                    self.w_done()
                self.store_tile(dst, s, ti, a, n, hb, htok, dtag, col0=HALO, tiles=PTILES)
            T.barrier()


def _rope_tables():
    t = np.arange(SEQ)
    r = (t // GRID_W).astype(np.float32)
    c = (t % GRID_W).astype(np.float32)
    axis_dim = HD // 2
    inv_freq = (10000.0 ** (-np.arange(0, axis_dim, 2, dtype=np.float32) / axis_dim)).astype(np.float32)
    ang = np.concatenate([r[:, None] * inv_freq[None], c[:, None] * inv_freq[None]], axis=-1)
    ang = np.concatenate([np.zeros((NMETA, HD // 2), np.float32), ang], axis=0)
    cos = np.cos(ang).astype(np.float32).T
    sin = np.sin(ang).astype(np.float32).T
    cosT = np.concatenate([cos, cos], axis=0)
    sinS = np.concatenate([-sin, sin], axis=0)
    return np.ascontiguousarray(cosT), np.ascontiguousarray(sinS)


def _inv_counts():
    t = np.arange(L)
    out = np.zeros((4, 128, L), np.float32)
    for g, w in enumerate(POOL_W):
        lo = np.clip(t - w // 2, 0, L)
        hi = np.clip(t - w // 2 + w, 0, L)
        cnt = (hi - lo).astype(np.float32)
        out[g] = (np.float32(1.0) / cnt)[None, :]
    return out


_PERM = np.concatenate([np.arange(0, HD, 2), np.arange(1, HD, 2)])


def _chunk_cols(v):
    return np.ascontiguousarray(np.asarray(v, np.float32).reshape(DC, 128).T)


def _prep_shared(inp):
    w_qkv = np.asarray(inp["w_qkv"], np.float32)
    nq = NH * HD
    colperm = np.arange(3072)
    for h in range(NH):
        colperm[h * HD:(h + 1) * HD] = h * HD + _PERM
    for h in range(NKV):
        colperm[nq + h * HD:nq + (h + 1) * HD] = nq + h * HD + _PERM
    w_qkv_p = np.ascontiguousarray(w_qkv[:, :, colperm])
    vecs = np.zeros((128, NV), np.float32)
    for j in range(2):
        vecs[:, V_ATTN + 16 * j:V_ATTN + 16 * j + 16] = _chunk_cols(inp["attn_norm"][j])
        vecs[:, V_POOLN + 16 * j:V_POOLN + 16 * j + 16] = _chunk_cols(inp["pool_norm"][j])
        vecs[:, V_POOLS + 16 * j:V_POOLS + 16 * j + 16] = _chunk_cols(inp["pool_scale"][j])
        vecs[:, V_QG + j] = np.asarray(inp["q_norm"], np.float32)[j][_PERM]
        vecs[:, V_KG + j] = np.asarray(inp["k_norm"], np.float32)[j][_PERM]
    for i in range(4):
        vecs[:, V_MLP + 16 * i:V_MLP + 16 * i + 16] = _chunk_cols(inp["mlp_norm"][i])
    vecs[:, V_FINAL:V_FINAL + 16] = _chunk_cols(inp["final_norm"])
    grep = np.zeros((4, 128, 128), np.float32)
    for j in range(2):
        grep[j] = np.asarray(inp["q_norm"], np.float32)[j][None, :]
        grep[2 + j] = np.asarray(inp["k_norm"], np.float32)[j][None, :]
    rot = np.zeros((128, 128), np.float32)
    for m in range(128):
        rot[(m + 64) % 128, m] = 1.0
    cosT, sinS = _rope_tables()
    return {
        "w_qkv": w_qkv_p,
        "w_o": np.ascontiguousarray(np.asarray(inp["w_o"], np.float32)),
        "w_pool": np.ascontiguousarray(np.asarray(inp["w_pool"], np.float32).reshape(2, 2048, 512)),
        "w_up": np.ascontiguousarray(np.asarray(inp["w_up"], np.float32)),
        "w_down": np.ascontiguousarray(np.asarray(inp["w_down"], np.float32)),
        "vecs": vecs, "grep": grep, "rot": rot, "cosT": cosT, "sinS": sinS, "icnt": _inv_counts(),
    }


def _prep_h0(xs, meta):
    out = np.empty((len(xs), D, L), np.float32)
    mT = np.asarray(meta, np.float32).T
    for i, x in enumerate(xs):
        out[i, :, :NMETA] = mT
        out[i, :, NMETA:] = np.asarray(x, np.float32).T
    return out


_CACHE = {}


def _get_prog(nseq, nlayers):
    key = (nseq, nlayers)
    if key not in _CACHE:
        p = Prog(nseq, nlayers)
        _CACHE[key] = p.build()
    return _CACHE[key]


def run_sequences(inp, seqs, ncores, nseq, nlayers=4, trace=False):
    shared = _prep_shared(inp)
    nc = _get_prog(nseq, nlayers)
    in_maps = []
    for c in range(ncores):
        m = dict(shared)
        m["h0"] = _prep_h0(seqs[c * nseq:(c + 1) * nseq], inp["meta_tokens"])
        in_maps.append(m)
    res = run_bass_kernel_spmd(nc, in_maps, core_ids=list(range(ncores)), trace=trace)
    outs = []
    for c in range(ncores):
        yT = res.results[c]["yT"]
        for k in range(nseq):
            outs.append(np.ascontiguousarray(yT[k].T))
    return outs, res


def kernel(**inputs):
    xp = np.asarray(inputs["x_prompt"], np.float32)
    xs = np.asarray(inputs["x_sample"], np.float32)
    seqs = [xp[b] for b in range(xp.shape[0])] + [xs[b] for b in range(xs.shape[0])]
    nseq = len(seqs) // NCORES
    outs, _ = run_sequences(inputs, seqs, NCORES, nseq)
    y_prompt = np.stack(outs[:xp.shape[0]], axis=0)
    y_sample = np.stack(outs[xp.shape[0]:], axis=0)
    return (y_prompt, y_sample)
```

```python
import os
from contextlib import ExitStack
import numpy as np
import concourse.bass as bass
import concourse.mybir as mybir
from concourse.bass_utils import run_bass_kernel_spmd

F32 = mybir.dt.float32
BF16 = mybir.dt.bfloat16
ALU = mybir.AluOpType
AF = mybir.ActivationFunctionType

D = 2048
DC = 16
FF = 8192
FC = 64
NH = 16
NKV = 4
HD = 128
NMETA = 16
SEQ = 2048
L = SEQ + NMETA
GRID_W = 64
EPS = 1e-6
NCORES = 8
POOL_W = (2, 4, 8, 16)
HALO = 8

V_ATTN = 0
V_POOLN = 32
V_POOLS = 64
V_MLP = 96
V_FINAL = 160
V_QG = 176
V_KG = 178
NV = 180


def _tiles(n_tiles=5):
    base = L // n_tiles
    rem = L - base * n_tiles
    out = []
    a = 0
    for i in range(n_tiles):
        t = base + (1 if i < rem else 0)
        out.append((a, t))
        a += t
    return out


TILES = _tiles(5)
PTILES = _tiles(8)
QTILES = _tiles(6)
QTM = max(t for _, t in QTILES)
PTM = max(t for _, t in PTILES)
TM = max(t for _, t in TILES)
KVW = 256
KV_TILES = [(i * KVW, KVW) for i in range(SEQ // KVW)] + [(2048, 16)]
KCHUNKS = [(i * 128, 128) for i in range(16)] + [(2048, 16)]


class Ev:
    __slots__ = ("key", "sem", "val", "eng")

    def __init__(self, key, sem, val, eng):
        self.key = key
        self.sem = sem
        self.val = val
        self.eng = eng


class Tracker:
    def __init__(self, nc, es, n_dma_sems=20):
        self.nc = nc
        self.eng = {"pe": nc.tensor, "act": nc.scalar, "dve": nc.vector, "pool": nc.gpsimd, "sp": nc.sync}
        self.csem = {k: es.enter_context(nc.semaphore("s_" + k)) for k in ("pe", "act", "dve", "pool")}
        self.cnt = {k: 0 for k in self.csem}
        self.seen = {k: {} for k in self.eng}
        self.dsem = {}
        self.dnext = {}
        for q in ("sp", "pool"):
            self.dsem[q] = [es.enter_context(nc.semaphore("d_%s%d" % (q, i))) for i in range(n_dma_sems)]
            self.dnext[q] = 0
        self.dcnt = {}
        self.tok = {}

    def _st(self, t):
        s = self.tok.get(t)
        if s is None:
            s = [{}, {}]
            self.tok[t] = s
        return s

    def _wait(self, e, ev):
        if self.seen[e].get(ev.key, 0) >= ev.val:
            return
        self.eng[e].wait_ge(ev.sem, ev.val)
        self.seen[e][ev.key] = ev.val

    def _deps(self, e, reads, writes, acc):
        for r in reads:
            for ev in self._st(r)[0].values():
                if ev.eng == e:
                    if e != "pe":
                        self._wait(e, ev)
                else:
                    self._wait(e, ev)
        for w in writes:
            st = self._st(w)
            evs = list(st[1].values())
            if not acc:
                evs += list(st[0].values())
            for ev in evs:
                if ev.eng == e:
                    continue
                self._wait(e, ev)

    def _commit(self, ev, reads, writes, acc):
        for w in writes:
            st = self._st(w)
            if acc and not st[1]:
                st[0][ev.key] = ev
            else:
                st[0] = {ev.key: ev}
                st[1] = {}
        for r in reads:
            if r in writes:
                continue
            self._st(r)[1][ev.key] = ev

    def op(self, e, fn, reads=(), writes=()):
        self._deps(e, reads, writes, False)
        ins = fn(self.eng[e])
        self.cnt[e] += 1
        ins.then_inc(self.csem[e], 1)
        ev = Ev(e, self.csem[e], self.cnt[e], e)
        self._commit(ev, reads, writes, False)
        return ev

    def dma(self, q, out, in_, reads=(), writes=(), acc=False, **kw):
        self._deps(q, reads, writes, acc)
        i = self.dnext[q]
        self.dnext[q] = (i + 1) % len(self.dsem[q])
        key = "d_%s%d" % (q, i)
        sem = self.dsem[q][i]
        prev = self.dcnt.get(key, 0)
        if prev:
            self._wait(q, Ev(key, sem, prev, None))
        ins = self.eng[q].dma_start(out=out, in_=in_, **kw)
        ins.then_inc(sem, 16)
        self.dcnt[key] = prev + 16
        ev = Ev(key, sem, prev + 16, None)
        self._commit(ev, reads, writes, acc)
        return ev

    def barrier(self, engines=("pe", "act", "dve", "pool", "sp")):
        evs = [Ev(k, self.csem[k], self.cnt[k], k) for k in self.csem if self.cnt[k] > 0]
        for q in ("sp",):
            for i, sem in enumerate(self.dsem[q]):
                key = "d_%s%d" % (q, i)
                if self.dcnt.get(key, 0):
                    evs.append(Ev(key, sem, self.dcnt[key], None))
        for e in engines:
            for ev in evs:
                if ev.eng == e:
                    continue
                self._wait(e, ev)
        self.tok = {k: v for k, v in self.tok.items() if isinstance(k, tuple) and k[0] == "wbf"}


class Prog:
    def __init__(self, nseq, nlayers, nslots=3):
        self.nseq = nseq
        self.nlayers = nlayers
        self.nslots = nslots
        self.uid = 0
        self.tiling = {}

    def sb(self, es, shape, dtype, name="t"):
        self.uid += 1
        return es.enter_context(self.nc.sbuf_tensor("%s_%d" % (name, self.uid), list(shape), dtype))

    def nb(self):
        b = self.bank_rr[self.bank_i % len(self.bank_rr)]
        self.bank_i += 1
        return b

    def plan_weights(self):
        sched = []
        for s in range(self.nseq):
            for i in range(self.nlayers):
                j = i // 2
                if i % 2 == 0:
                    for _ in KV_TILES:
                        sched.append(("qkv", j, 4))
                        sched.append(("qkv", j, 5))
                    for _ in QTILES:
                        for b in range(4):
                            sched.append(("qkv", j, b))
                        for b in range(4):
                            sched.append(("wo", j, b))
                else:
                    for _ in PTILES:
                        for g in range(4):
                            sched.append(("pool", j, g))
                for _ in TILES:
                    for fb in range(16):
                        sched.append(("up", i, fb))
                    for jq in range(4):
                        for fb4 in range(4):
                            sched.append(("down", i, jq, fb4))
        self.sched = sched
        self.w_next_load = 0
        self.w_cur = -1

    def _w_src(self, key):
        kind = key[0]
        if kind == "qkv":
            _, j, b = key
            return (self.wb["qkv"][j][:, b * 512:(b + 1) * 512].rearrange("(kc p) n -> p kc n", p=128), 16,
                    ("wbf", "qkv", j))
        if kind == "wo":
            _, j, b = key
            return (self.wb["wo"][j][:, b * 512:(b + 1) * 512].rearrange("(kc p) n -> p kc n", p=128), 16,
                    ("wbf", "wo", j))
        if kind == "pool":
            _, j, g = key
            return (self.wb["pool"][j][g * 512:(g + 1) * 512, :].rearrange("(kc p) n -> p kc n", p=128), 4,
                    ("wbf", "pool", j))
        if kind == "up":
            _, i, fb = key
            return (self.wb["up"][i][:, fb * 512:(fb + 1) * 512].rearrange("(kc p) n -> p kc n", p=128), 16,
                    ("wbf", "up", i))
        _, i, jq, fb4 = key
        return (self.wb["down"][i][fb4 * 2048:(fb4 + 1) * 2048, jq * 512:(jq + 1) * 512]
                .rearrange("(kc p) n -> p kc n", p=128), 16, ("wbf", "down", i))

    def _w_emit_load(self):
        i = self.w_next_load
        if i >= len(self.sched):
            return
        src, nk, tok = self._w_src(self.sched[i])
        sl = i % self.nslots
        self.T.dma("sp", out=self.wslot[sl][:, 0:nk, :], in_=src, reads=[tok], writes=[("ws", sl)])
        self.w_next_load += 1

    def w_get(self, key):
        self.w_cur += 1
        assert self.sched[self.w_cur] == key, (self.sched[self.w_cur], key)
        while self.w_next_load <= self.w_cur:
            self._w_emit_load()
        sl = self.w_cur % self.nslots
        return self.wslot[sl], ("ws", sl)

    def w_done(self):
        while self.w_next_load < min(len(self.sched), self.w_cur + self.nslots + 1):
            self._w_emit_load()

    def cast_w(self, kind, j):
        src = self.w32[kind][j]
        dst = self.wb[kind][j]
        rows, cols = src.shape
        step = max(1, (4 * 1024 * 1024) // cols)
        for r0 in range(0, rows, step):
            r1 = min(rows, r0 + step)
            self.T.dma("pool", out=dst[r0:r1, :], in_=src[r0:r1, :], writes=[("wbf", kind, j)], acc=True,
                       max_dma_last_dim=2048)

    def lazy_casts(self, s, phase, j):
        if s != 0:
            return
        nl = self.nlayers
        todo = []
        if phase == "q":
            i = 2 * j
            todo += [("up", i), ("down", i)]
            if i + 1 < nl:
                todo += [("pool", j), ("up", i + 1), ("down", i + 1)]
        elif phase == "pool":
            if 2 * j + 2 < nl:
                todo += [("qkv", j + 1), ("wo", j + 1)]
        for kind, idx in todo:
            self.cast_w(kind, idx)

    def build(self):
        nc = bass.Bass("TRN2", target_bir_lowering=False)
        self.nc = nc
        nseq = self.nseq
        dt = nc.dram_tensor
        self.h0 = dt("h0", [nseq, D, L], F32, kind="ExternalInput").ap()
        self.yT = dt("yT", [nseq, D, SEQ], F32, kind="ExternalOutput").ap()
        w32 = {
            "qkv": dt("w_qkv", [2, D, 3072], F32, kind="ExternalInput").ap(),
            "wo": dt("w_o", [2, D, D], F32, kind="ExternalInput").ap(),
            "pool": dt("w_pool", [2, 2048, 512], F32, kind="ExternalInput").ap(),
            "up": dt("w_up", [4, D, FF], F32, kind="ExternalInput").ap(),
            "down": dt("w_down", [4, FF, D], F32, kind="ExternalInput").ap(),
        }
        self.wb = {
            "qkv": dt("wb_qkv", [2, D, 3072], BF16, kind="Internal").ap(),
            "wo": dt("wb_o", [2, D, D], BF16, kind="Internal").ap(),
            "pool": dt("wb_pool", [2, 2048, 512], BF16, kind="Internal").ap(),
            "up": dt("wb_up", [4, D, FF], BF16, kind="Internal").ap(),
            "down": dt("wb_down", [4, FF, D], BF16, kind="Internal").ap(),
        }
        vecs_d = dt("vecs", [128, NV], F32, kind="ExternalInput").ap()
        grep_d = dt("grep", [4, 128, 128], F32, kind="ExternalInput").ap()
        rot_d = dt("rot", [128, 128], F32, kind="ExternalInput").ap()
        cos_d = dt("cosT", [128, L], F32, kind="ExternalInput").ap()
        sin_d = dt("sinS", [128, L], F32, kind="ExternalInput").ap()
        self.cos_d, self.sin_d = cos_d, sin_d
        self.icnt_d = dt("icnt", [4, 128, L], F32, kind="ExternalInput").ap()
        self.hbuf = [dt("hA", [nseq, D, L], F32, kind="Internal").ap(),
                     dt("hB", [nseq, D, L], F32, kind="Internal").ap()]

        with ExitStack() as es:
            T = Tracker(nc, es)
            self.T = T
            self.pst = es.enter_context(nc.psum_tensor("pst", [128, 8, 512], F32))
            self.ps = [self.pst[:, i, :] for i in range(8)]
            self.bank_rr = list(range(8))
            self.bank_i = 0
            self.wslot = [self.sb(es, [128, 16, 512], BF16, "ws") for _ in range(self.nslots)]
            self.vecs = self.sb(es, [128, NV], F32, "vecs")
            self.rot = self.sb(es, [128, 128], F32, "rot")
            self.onesD = self.sb(es, [128, 128], BF16, "onesD")
            self.onesH = self.sb(es, [128, 128], BF16, "onesH")
            self.ones1 = self.sb(es, [128, 128], BF16, "ones1")
            self.epst = self.sb(es, [128, 1], F32, "eps")
            self.negb = self.sb(es, [128, 2], F32, "negb")
            gtmp = self.sb(es, [128, 4, 128], F32, "gtmp")
            gmax = self.sb(es, [128, 4], F32, "gmax")

            self.w32 = w32
            self.cast_w("qkv", 0)
            self.cast_w("wo", 0)
            T.dma("sp", out=self.vecs[:], in_=vecs_d[:, :], writes=["vecs"])
            T.dma("sp", out=self.rot[:], in_=rot_d[:, :], writes=["rot"])
            T.dma("sp", out=gtmp[:], in_=grep_d.rearrange("g p n -> p g n"), writes=["gtmp"])
            T.op("dve", lambda e: e.memset(self.onesD[:], 1.0 / D), writes=["onesD"])
            T.op("dve", lambda e: e.memset(self.onesH[:], 1.0 / HD), writes=["onesH"])
            T.op("dve", lambda e: e.memset(self.ones1[:], 1.0), writes=["ones1"])
            T.op("dve", lambda e: e.memset(self.epst[:], EPS), writes=["eps"])
            T.op("dve", lambda e: e.tensor_reduce(out=gmax[:], in_=gtmp[:], axis=mybir.AxisListType.X, op=ALU.max, apply_absolute_value=True),
                 reads=["gtmp"], writes=["gmax"])
            for j in range(2):
                T.op("dve", lambda e, j=j: e.scalar_tensor_tensor(
                    out=self.negb[:, j:j + 1], in0=gmax[:, j:j + 1], scalar=-float(np.sqrt(HD)),
                    in1=gmax[:, 2 + j:3 + j], op0=ALU.mult, op1=ALU.mult), reads=["gmax"], writes=["negb"])
            T.barrier()

            self.plan_weights()
            for s in range(nseq):
                sl = 0
                for i in range(self.nlayers):
                    j = i // 2
                    src = self.h0 if sl == 0 else self.hbuf[(sl - 1) % 2]
                    dst = self.hbuf[sl % 2]
                    stag = ("h0",) if sl == 0 else ("hb", (sl - 1) % 2)
                    dtag = ("hb", sl % 2)
                    if i % 2 == 0:
                        self.attn_layer(s, j, src, dst, stag, dtag)
                    else:
                        self.pool_layer(s, j, src, dst, stag, dtag)
                    sl += 1
                    src = self.hbuf[(sl - 1) % 2]
                    dst = self.hbuf[sl % 2]
                    stag = ("hb", (sl - 1) % 2)
                    dtag = ("hb", sl % 2)
                    self.mlp_layer(s, i, src, dst, stag, dtag, final=(i == self.nlayers - 1))
                    sl += 1
            T.barrier()
        return nc

    def load_tile(self, src, s, a, n, hb, htok, stag, col0=0, ngroups=1):
        T = self.T
        wt = self.tiling.get(stag + (s,), TILES)
        rd = [stag + (s, ti) for ti, (ta, tt) in enumerate(wt) if ta < a + n and ta + tt > a]
        gs = DC // ngroups
        for g in range(ngroups):
            T.dma("sp", out=hb[:, g * gs:(g + 1) * gs, col0:col0 + n],
                  in_=src[s][g * gs * 128:(g + 1) * gs * 128, a:a + n].rearrange("(c p) t -> p c t", p=128),
                  reads=rd, writes=[(htok, c) for c in range(g * gs, (g + 1) * gs)])

    def rmsnorm_a(self, hb, htok, n, sq, col0=0, sqtoks=("sq",)):
        hr = [(htok, c) for c in range(DC)]
        self.T.op("act", lambda e: e.activation(out=sq[:, :, 0:n], in_=hb[:, :, col0:col0 + n], func=AF.Square),
                  reads=hr, writes=list(sqtoks))

    def rmsnorm_b(self, hb, htok, n, gcol, sq, rstd, out, otok, col0=0, sqtoks=("sq",), engs=("dve",), rtok="rstd"):
        T = self.T
        b = self.nb()
        ps = self.ps[b]

        def mm(e):
            for c in range(DC):
                ins = e.matmul(ps[:, 0:n], lhsT=self.onesD[:, :], rhs=sq[:, c, 0:n], start=(c == 0), stop=(c == DC - 1))
            return ins
        T.op("pe", mm, reads=list(sqtoks) + ["onesD"], writes=[("ps", b)])
        T.op("act", lambda e: e.activation(out=rstd[:, 0:n], in_=ps[:, 0:n], func=AF.Ln, bias=self.epst[:, 0:1], scale=1.0),
             reads=[("ps", b), "eps"], writes=[rtok])
        T.op("act", lambda e: e.activation(out=rstd[:, 0:n], in_=rstd[:, 0:n], func=AF.Exp, scale=-0.5), reads=[rtok], writes=[rtok])
        for c in range(DC):
            T.op(engs[c % len(engs)], lambda e, c=c: e.scalar_tensor_tensor(
                out=out[:, c, 0:n], in0=hb[:, c, col0:col0 + n], scalar=self.vecs[:, gcol + c:gcol + c + 1],
                in1=rstd[:, 0:n], op0=ALU.mult, op1=ALU.mult),
                reads=[(htok, c), rtok, "vecs"], writes=[(otok, c)])

    def rmsnorm(self, hb, htok, n, gcol, sq, rstd, out, otok, col0=0, sqtoks=("sq",), engs=("dve",), rtok="rstd"):
        self.rmsnorm_a(hb, htok, n, sq, col0, sqtoks)
        self.rmsnorm_b(hb, htok, n, gcol, sq, rstd, out, otok, col0, sqtoks, engs, rtok)

    def store_tile(self, dst, s, ti, a, n, hb, htok, dtag, col0=0, tiles=None, ngroups=1):
        self.tiling[dtag + (s,)] = TILES if tiles is None else tiles
        gs = DC // ngroups
        for g in range(ngroups):
            self.T.dma("sp", out=dst[s][g * gs * 128:(g + 1) * gs * 128, a:a + n].rearrange("(c p) t -> p c t", p=128),
                       in_=hb[:, g * gs:(g + 1) * gs, col0:col0 + n],
                       reads=[(htok, c) for c in range(g * gs, (g + 1) * gs)], writes=[dtag + (s, ti)], acc=(ngroups > 1))

    def store_tile_group(self, dst, s, ti, a, n, hb, htok, dtag, g, gs=4):
        self.tiling[dtag + (s,)] = QTILES
        self.T.dma("sp", out=dst[s][g * gs * 128:(g + 1) * gs * 128, a:a + n].rearrange("(c p) t -> p c t", p=128),
                   in_=hb[:, g * gs:(g + 1) * gs, 0:n],
                   reads=[(htok, c) for c in range(g * gs, (g + 1) * gs)], writes=[dtag + (s, ti)], acc=True)

    def mlp_layer(self, s, i, src, dst, stag, dtag, final):
        T = self.T
        with ExitStack() as es:
            hT = [self.sb(es, [128, DC, TM], F32, "hT") for _ in range(2)]
            sq = self.sb(es, [128, DC, TM], BF16, "sq")
            hn = self.sb(es, [128, DC, TM], BF16, "hn")
            rstd = self.sb(es, [128, TM], F32, "rstd")
            rl = [self.sb(es, [128, TM], F32, "rl") for _ in range(3)]
            uT = self.sb(es, [128, FC, TM], BF16, "uT")
            if final:
                sq2 = uT[:, 32:48, :]
                sq2toks = [("uT", f) for f in range(32, 48)]
                rstd2 = self.sb(es, [128, TM], F32, "rstd2")
            def prep_a(ti):
                a, n = TILES[ti]
                self.load_tile(src, s, a, n, hT[ti % 2], "hT%d" % (ti % 2), stag)
                self.rmsnorm_a(hT[ti % 2], "hT%d" % (ti % 2), n, sq)

            def prep_b(ti):
                a, n = TILES[ti]
                self.rmsnorm_b(hT[ti % 2], "hT%d" % (ti % 2), n, V_MLP + 16 * i, sq, rstd, hn, "hn")

            prep_a(0)
            prep_b(0)
            for ti, (a, n) in enumerate(TILES):
                hb = hT[ti % 2]
                htok = "hT%d" % (ti % 2)
                hnr = [("hn", c) for c in range(DC)]
                for fb in range(16):
                    slot, stok = self.w_get(("up", i, fb))
                    for fc in range(4):
                        f = fb * 4 + fc
                        b = self.nb()
                        ps = self.ps[b]

                        def mm(e, fc=fc, ps=ps, slot=slot):
                            for kc in range(DC):
                                ins = e.matmul(ps[:, 0:n], lhsT=slot[:, kc, fc * 128:(fc + 1) * 128], rhs=hn[:, kc, 0:n],
                                               start=(kc == 0), stop=(kc == DC - 1))
                            return ins
                        T.op("pe", mm, reads=[stok] + hnr, writes=[("ps", b)])
                        r = rl[f % 3]
                        T.op("act", lambda e, r=r, ps=ps: e.activation(out=r[:, 0:n], in_=ps[:, 0:n], func=AF.Relu),
                             reads=[("ps", b)], writes=[("rl", f % 3)])
                        T.op("dve", lambda e, r=r, f=f: e.tensor_tensor(out=uT[:, f, 0:n], in0=r[:, 0:n], in1=r[:, 0:n], op=ALU.mult),
                             reads=[("rl", f % 3)], writes=[("uT", f)])
                    self.w_done()
                if ti + 1 < len(TILES):
                    prep_a(ti + 1)
                for jq in range(4):
                    if jq == 1 and ti + 1 < len(TILES):
                        prep_b(ti + 1)
                    banks = [4 * (jq % 2) + dd for dd in range(4)]
                    for fb4 in range(4):
                        slot, stok = self.w_get(("down", i, jq, fb4))

                        def mm(e, slot=slot, fb4=fb4, banks=banks):
                            for fl in range(16):
                                f = fb4 * 16 + fl
                                for dd in range(4):
                                    ins = e.matmul(self.ps[banks[dd]][:, 0:n], lhsT=slot[:, fl, dd * 128:(dd + 1) * 128],
                                                   rhs=uT[:, f, 0:n], start=(f == 0), stop=(f == FC - 1))
                            return ins
                        T.op("pe", mm, reads=[stok] + [("uT", fb4 * 16 + fl) for fl in range(16)],
                             writes=[("ps", b) for b in banks])
                        self.w_done()
                    for dd in range(4):
                        c = jq * 4 + dd
                        b = banks[dd]
                        T.op("dve", lambda e, c=c, b=b: e.tensor_tensor(out=hb[:, c, 0:n], in0=self.ps[b][:, 0:n], in1=hb[:, c, 0:n], op=ALU.add),
                             reads=[("ps", b), (htok, c)], writes=[(htok, c)])
                if final:
                    self.rmsnorm(hb, htok, n, V_FINAL, sq2, rstd2, hb, htok, sqtoks=sq2toks, rtok="rstd2")
                    lo = max(a, NMETA)
                    T.dma("sp", out=self.yT[s][:, lo - NMETA:a + n - NMETA].rearrange("(c p) t -> p c t", p=128),
                          in_=hb[:, :, lo - a:n], reads=[(htok, c) for c in range(DC)], writes=[("y", s, ti)])
                else:
                    self.store_tile(dst, s, ti, a, n, hb, htok, dtag)
            T.barrier()

    def qk_pipeline(self, groups, n, a_off, cs, sn, sets, fillers=(), peng="pool", cstoks=("cs", "sn")):
        T = self.T
        G = len(groups)
        PA = [(0, 1), (2, 3)]
        SB = [(4, 5), (6, 7)]
        fillers = list(fillers)
        ns = len(sets)
        for it in range(G + 2):
            deferred = None
            g = it
            if g < G:
                pa = PA[g % 2]
                k = g % ns
                xraw, sqh, rsh, xn = sets[k]
                groups[g]["proj"](pa)
                pv = self.pst[:, pa[0]:pa[0] + 2, 0:n]
                prd = [("ps", pa[0]), ("ps", pa[1])]
                T.op("act", lambda e: e.copy(out=xraw[:, :, 0:n], in_=pv), reads=prd, writes=[("xraw", k)])
                T.op("act", lambda e: e.activation(out=sqh[:, :, 0:n], in_=pv, func=AF.Square), reads=prd, writes=[("sqh", k)])
            else:
                for _ in range(2):
                    if fillers:
                        fillers.pop(0)()
            g = it - 1
            if 0 <= g < G:
                sbk = SB[g % 2]
                k = g % ns
                xraw, sqh, rsh, xn = sets[k]
                gcol = groups[g]["gcol"]

                def st(e, sbk=sbk, sqh=sqh):
                    for i in range(2):
                        ins = e.matmul(self.ps[sbk[i]][:, 0:n], lhsT=self.onesH[:, :], rhs=sqh[:, i, 0:n], start=True, stop=True)
                    return ins
                T.op("pe", st, reads=[("sqh", k), "onesH"], writes=[("ps", sbk[0]), ("ps", sbk[1])])
                sv = self.pst[:, sbk[0]:sbk[0] + 2, 0:n]
                T.op("act", lambda e: e.activation(out=rsh[:, :, 0:n], in_=sv, func=AF.Ln, bias=self.epst[:, 0:1], scale=1.0),
                     reads=[("ps", sbk[0]), ("ps", sbk[1]), "eps"], writes=[("rsh", k)])
                T.op("act", lambda e: e.activation(out=rsh[:, :, 0:n], in_=rsh[:, :, 0:n], func=AF.Exp, scale=-0.5),
                     reads=[("rsh", k)], writes=[("rsh", k)])
                deferred = (xraw, rsh, xn, gcol, k)
            g = it - 2
            if 0 <= g < G:
                sbk = SB[g % 2]
                k = g % ns
                xraw, sqh, rsh, xn = sets[k]

                def rt(e, sbk=sbk, xn=xn):
                    for i in range(2):
                        ins = e.matmul(self.ps[sbk[i]][:, 0:n], lhsT=self.rot[:, :], rhs=xn[:, i, 0:n], start=True, stop=True)
                    return ins
                T.op("pe", rt, reads=[("xn", k), "rot"], writes=[("ps", sbk[0]), ("ps", sbk[1])])
                for i in range(2):
                    T.op(peng, lambda e, i=i: e.tensor_tensor(out=xraw[:, i, 0:n], in0=xn[:, i, 0:n], in1=cs[:, 0:n], op=ALU.mult),
                         reads=[("xn", k), cstoks[0]], writes=[("xraw", k)])
                for i in range(2):
                    T.op("dve", lambda e, i=i: e.tensor_tensor(out=rsh[:, i, 0:n], in0=self.ps[sbk[i]][:, 0:n], in1=sn[:, 0:n], op=ALU.mult),
                         reads=[("ps", sbk[i]), cstoks[1]], writes=[("rsh", k)])
                for i in range(2):
                    oap, otok = groups[g]["outs"][i]
                    T.op(peng, lambda e, i=i, oap=oap: e.tensor_tensor(out=oap, in0=xraw[:, i, 0:n], in1=rsh[:, i, 0:n], op=ALU.add),
                         reads=[("xraw", k), ("rsh", k)], writes=[otok])
            if deferred is not None:
                xraw, rsh, xn, gcol, k = deferred
                T.op("dve", lambda e: e.scalar_tensor_tensor(out=xn[:, :, 0:n], in0=xraw[:, :, 0:n], scalar=self.vecs[:, gcol:gcol + 1],
                                                             in1=rsh[:, :, 0:n], op0=ALU.mult, op1=ALU.mult),
                     reads=[("xraw", k), ("rsh", k), "vecs"], writes=[("xn", k)])
        while fillers:
            fillers.pop(0)()

    def attn_layer(self, s, j, src, dst, stag, dtag):
        T = self.T
        scale = float(HD ** -0.5)
        with ExitStack() as es:
            kT = self.sb(es, [128, NKV, L], BF16, "kT")
            vS = self.sb(es, [128, 17, 512], BF16, "vS")
            rstd_all = self.sb(es, [128, L], F32, "rstda")
            with ExitStack() as es2:
                hbs = [self.sb(es2, [128, DC, KVW], F32, "hTk") for _ in range(2)]
                sq = self.sb(es2, [128, DC, KVW], BF16, "sqk")
                hns = [self.sb(es2, [128, DC, KVW], BF16, "hnk") for _ in range(2)]
                css = [(self.sb(es2, [128, KVW], F32, "csk"), self.sb(es2, [128, KVW], F32, "snk")) for _ in range(2)]
                sets = [(self.sb(es2, [128, 2, KVW], F32, "xraw"), self.sb(es2, [128, 2, KVW], BF16, "sqh"),
                         self.sb(es2, [128, 2, KVW], F32, "rsh"), self.sb(es2, [128, 2, KVW], F32, "xn")) for _ in range(3)]
                peng = "dve"

                def kv_prep_a(ti):
                    a, n = KV_TILES[ti]
                    p = ti % 2
                    self.load_tile(src, s, a, n, hbs[p], "hT%d" % p, stag)
                    T.dma("sp", out=css[p][0][:, 0:n], in_=self.cos_d[:, a:a + n], writes=["cs%d" % p])
                    T.dma("sp", out=css[p][1][:, 0:n], in_=self.sin_d[:, a:a + n], writes=["sn%d" % p])
                    self.rmsnorm_a(hbs[p], "hT%d" % p, n, sq)

                def kv_prep_b(ti):
                    a, n = KV_TILES[ti]
                    p = ti % 2
                    self.rmsnorm_b(hbs[p], "hT%d" % p, n, V_ATTN + 16 * j, sq, rstd_all[:, a:a + n], hns[p], "hn%d" % p,
                                   rtok=("rstda", ti))

                kv_prep_a(0)
                kv_prep_b(0)
                for ti, (a, n) in enumerate(KV_TILES):
                    p = ti % 2
                    hn = hns[p]
                    hnr = [("hn%d" % p, c) for c in range(DC)]
                    wst = {}
                    if ti + 1 < len(KV_TILES):
                        kv_prep_a(ti + 1)

                    def kproj(pa, g, n=n, wst=wst, hnr=hnr, hn=hn):
                        if g == 0:
                            wst["k"] = self.w_get(("qkv", j, 4))
                        slot, stok = wst["k"]
                        for i in range(2):
                            kvh = 2 * g + i
                            b = pa[i]

                            def mm(e, kvh=kvh, b=b):
                                for kc in range(DC):
                                    ins = e.matmul(self.ps[b][:, 0:n], lhsT=slot[:, kc, kvh * 128:(kvh + 1) * 128], rhs=hn[:, kc, 0:n],
                                                   start=(kc == 0), stop=(kc == DC - 1))
                                return ins
                            T.op("pe", mm, reads=[stok] + hnr, writes=[("ps", b)])
                        if g == 1:
                            self.w_done()

                    groups = [dict(proj=(lambda pa, g=g: kproj(pa, g)), gcol=V_KG + j,
                                   outs=[(kT[:, 2 * g + i, a:a + n], ("kT", 2 * g + i, ti)) for i in range(2)]) for g in range(2)]
                    nm = (n + 127) // 128
                    fills = []
                    for m in range(nm):
                        def vfill(m=m, n=n, a=a, nm=nm, wst=wst, hnr=hnr, hn=hn):
                            if m == 0:
                                wst["v"] = self.w_get(("qkv", j, 5))
                            slot, stok = wst["v"]
                            ms = min(128, n - m * 128)
                            b = self.nb()

                            def mm(e):
                                for kc in range(DC):
                                    ins = e.matmul(self.ps[b][0:ms, 0:512], lhsT=hn[:, kc, m * 128:m * 128 + ms], rhs=slot[:, kc, :],
                                                   start=(kc == 0), stop=(kc == DC - 1))
                                return ins
                            T.op("pe", mm, reads=[stok] + hnr, writes=[("ps", b)])
                            kc_i = a // 128 + m
                            T.op("act", lambda e: e.copy(out=vS[0:ms, kc_i, :], in_=self.ps[b][0:ms, 0:512]),
                                 reads=[("ps", b)], writes=[("vS", kc_i)])
                            if m == nm - 1:
                                self.w_done()
                        fills.append(vfill)
                    if ti + 1 < len(KV_TILES):
                        fills.append(lambda ti=ti: kv_prep_b(ti + 1))
                    self.qk_pipeline(groups, n, a, css[p][0], css[p][1], sets, fills, peng=peng, cstoks=("cs%d" % p, "sn%d" % p))
                T.barrier()
            self.lazy_casts(s, "q", j)
            with ExitStack() as es2:
                TM_ = QTM
                hbq = [self.sb(es2, [128, DC, TM_], F32, "hTq") for _ in range(2)]
                hn = self.sb(es2, [128, DC, TM_], BF16, "hnq")
                qT = self.sb(es2, [128, NH, TM_], BF16, "qT")
                oT = self.sb(es2, [128, NH, TM_], BF16, "oT")
                P = [self.sb(es2, [128, TM_], BF16, "P") for _ in range(4)]
                rcp = [self.sb(es2, [128, TM_], F32, "rcp") for _ in range(2)]
                sets = [(self.sb(es2, [128, 2, TM_], F32, "xraw"), self.sb(es2, [128, 2, TM_], BF16, "sqh"),
                         self.sb(es2, [128, 2, TM_], F32, "rsh"), self.sb(es2, [128, 2, TM_], F32, "xn")) for _ in range(3)]
                csq = [(self.sb(es2, [128, TM_], F32, "csq"), self.sb(es2, [128, TM_], F32, "snq")) for _ in range(2)]
                otoks = [("oT", h) for h in range(NH)]
                gcol = V_ATTN + 16 * j

                def q_load(ti):
                    a, n = QTILES[ti]
                    p = ti % 2
                    self.load_tile(src, s, a, n, hbq[p], "hTq%d" % p, stag, ngroups=4)
                    T.dma("sp", out=csq[p][0][:, 0:n], in_=self.cos_d[:, a:a + n], writes=["cs%d" % p])
                    T.dma("sp", out=csq[p][1][:, 0:n], in_=self.sin_d[:, a:a + n], writes=["sn%d" % p])

                def q_norm(ti):
                    a, n = QTILES[ti]
                    p = ti % 2
                    for c in range(DC):
                        T.op("dve", lambda e, c=c: e.scalar_tensor_tensor(
                            out=hn[:, c, 0:n], in0=hbq[p][:, c, 0:n], scalar=self.vecs[:, gcol + c:gcol + c + 1],
                            in1=rstd_all[:, a:a + n], op0=ALU.mult, op1=ALU.mult),
                            reads=[("hTq%d" % p, c), "vecs"], writes=[("hn", c)])

                q_load(0)
                q_norm(0)
                for ti, (a, n) in enumerate(QTILES):
                    p = ti % 2
                    hb, htok = hbq[p], "hTq%d" % p
                    cs, sn = csq[p]
                    hnr = [("hn", c) for c in range(DC)]
                    wst = {}

                    def qproj(pa, g, n=n, wst=wst, hnr=hnr):
                        if g % 2 == 0:
                            wst["q"] = self.w_get(("qkv", j, g // 2))
                        slot, stok = wst["q"]
                        for i in range(2):
                            hh = (2 * g + i) % 4
                            b = pa[i]

                            def mm(e, hh=hh, b=b):
                                for kc in range(DC):
                                    ins = e.matmul(self.ps[b][:, 0:n], lhsT=slot[:, kc, hh * 128:(hh + 1) * 128], rhs=hn[:, kc, 0:n],
                                                   start=(kc == 0), stop=(kc == DC - 1))
                                return ins
                            T.op("pe", mm, reads=[stok] + hnr, writes=[("ps", b)])
                        if g % 2 == 1:
                            self.w_done()

                    groups = [dict(proj=(lambda pa, g=g: qproj(pa, g)), gcol=V_QG + j,
                                   outs=[(qT[:, 2 * g + i, 0:n], ("qT", 2 * g + i)) for i in range(2)]) for g in range(NH // 2)]
                    self.qk_pipeline(groups, n, a, cs, sn, sets, peng="dve", cstoks=("cs%d" % p, "sn%d" % p))
                    if ti + 1 < len(QTILES):
                        q_load(ti + 1)
                        q_norm(ti + 1)
                    steps = [(h, kc) for h in range(NH) for kc in range(len(KCHUNKS))]
                    sbank = {}

                    def S(idx):
                        h, kc = steps[idx]
                        kvh = h // 4
                        k0, ks = KCHUNKS[kc]
                        b = idx % 3
                        sbank[idx] = b
                        T.op("pe", lambda e: e.matmul(self.ps[b][0:ks, 0:n], lhsT=kT[:, kvh, k0:k0 + ks], rhs=qT[:, h, 0:n], start=True, stop=True),
                             reads=[("qT", h)], writes=[("ps", b)])

                    S(0)
                    S(1)
                    for idx, (h, kc) in enumerate(steps):
                        kvh = h // 4
                        k0, ks = KCHUNKS[kc]
                        b = sbank[idx]
                        pi = idx % 4
                        ob = 3 + (h % 2)
                        sb_ = 5 + (h % 2)
                        T.op("act", lambda e, b=b, pi=pi, ks=ks: e.activation(out=P[pi][0:ks, 0:n], in_=self.ps[b][0:ks, 0:n], func=AF.Exp,
                                                                             bias=self.negb[0:ks, j:j + 1], scale=scale),
                             reads=[("ps", b), "negb"], writes=[("P", pi)])
                        if idx + 2 < len(steps):
                            S(idx + 2)
                        last = (kc == len(KCHUNKS) - 1)

                        def pv(e, pi=pi, ks=ks, kc=kc, kvh=kvh, ob=ob, sb_=sb_, last=last):
                            e.matmul(self.ps[ob][:, 0:n], lhsT=vS[0:ks, kc, kvh * 128:(kvh + 1) * 128], rhs=P[pi][0:ks, 0:n],
                                     start=(kc == 0), stop=last)
                            return e.matmul(self.ps[sb_][:, 0:n], lhsT=self.ones1[0:ks, :], rhs=P[pi][0:ks, 0:n],
                                            start=(kc == 0), stop=last)
                        T.op("pe", pv, reads=[("P", pi), "ones1"], writes=[("ps", ob), ("ps", sb_)])
                        if last:
                            rc = rcp[h % 2]
                            T.op("dve", lambda e, rc=rc, sb_=sb_: e.reciprocal(out=rc[:, 0:n], in_=self.ps[sb_][:, 0:n]),
                                 reads=[("ps", sb_)], writes=[("rcp", h % 2)])
                            T.op("dve", lambda e, rc=rc, ob=ob, h=h: e.tensor_tensor(out=oT[:, h, 0:n], in0=self.ps[ob][:, 0:n], in1=rc[:, 0:n], op=ALU.mult),
                                 reads=[("ps", ob), ("rcp", h % 2)], writes=[("oT", h)])
                    for ob_ in range(4):
                        slot, stok = self.w_get(("wo", j, ob_))
                        for dd in range(4):
                            d = ob_ * 4 + dd
                            b = self.nb()

                            def mm(e, dd=dd, b=b, slot=slot):
                                for c in range(NH):
                                    ins = e.matmul(self.ps[b][:, 0:n], lhsT=slot[:, c, dd * 128:(dd + 1) * 128], rhs=oT[:, c, 0:n],
                                                   start=(c == 0), stop=(c == NH - 1))
                                return ins
                            T.op("pe", mm, reads=[stok] + otoks, writes=[("ps", b)])
                            T.op("dve", lambda e, d=d, b=b: e.tensor_tensor(out=hb[:, d, 0:n], in0=self.ps[b][:, 0:n], in1=hb[:, d, 0:n], op=ALU.add),
                                 reads=[("ps", b), (htok, d)], writes=[(htok, d)])
                        self.store_tile_group(dst, s, ti, a, n, hb, htok, dtag, ob_)
                        self.w_done()
                T.barrier()

    def pool_layer(self, s, j, src, dst, stag, dtag):
        T = self.T
        W = PTM + 2 * HALO
        U0 = HALO
        self.lazy_casts(s, "pool", j)
        with ExitStack() as es:
            hbs = [self.sb(es, [128, DC, W], F32, "hTp") for _ in range(2)]
            sq = self.sb(es, [128, DC, W], BF16, "sqp")
            xf = self.sb(es, [128, DC, W], F32, "xf")
            rstds = [self.sb(es, [128, W], F32, "rstdp") for _ in range(2)]
            A = self.sb(es, [128, 16, W], F32, "pA")
            B = self.sb(es, [128, 12, W], F32, "pB")
            ic = self.sb(es, [128, 4, 2 * HALO], F32, "ic")
            E = self.sb(es, [128, DC, HALO], F32, "pE")
            mx = self.sb(es, [128, DC, PTM], BF16, "mx")

            def geom(ti):
                a, n = PTILES[ti]
                lo = max(0, a - HALO)
                hi = min(L, a + n + HALO)
                return a, n, lo, hi - lo, lo - (a - HALO)

            def prep(ti):
                a, n, lo, nl, c0 = geom(ti)
                p = ti % 2
                hb, htok = hbs[p], "hT%d" % p
                self.load_tile(src, s, lo, nl, hb, htok, stag, col0=c0)
                self.rmsnorm_a(hb, htok, nl, sq, col0=c0)
                b = self.nb()
                ps = self.ps[b]

                def mm(e):
                    for c in range(DC):
                        ins = e.matmul(ps[:, 0:nl], lhsT=self.onesD[:, :], rhs=sq[:, c, 0:nl], start=(c == 0), stop=(c == DC - 1))
                    return ins
                T.op("pe", mm, reads=["sq", "onesD"], writes=[("ps", b)])
                rstd, rtok = rstds[p], "rstd%d" % p
                T.op("act", lambda e: e.activation(out=rstd[:, 0:nl], in_=ps[:, 0:nl], func=AF.Ln, bias=self.epst[:, 0:1], scale=1.0),
                     reads=[("ps", b), "eps"], writes=[rtok])
                T.op("act", lambda e: e.activation(out=rstd[:, 0:nl], in_=rstd[:, 0:nl], func=AF.Exp, scale=-0.5), reads=[rtok], writes=[rtok])

            prep(0)
            for ti in range(len(PTILES)):
                a, n, lo, nl, c0 = geom(ti)
                p = ti % 2
                hb, htok = hbs[p], "hT%d" % p
                rstd, rtok = rstds[p], "rstd%d" % p
                xtok = [("xf", c) for c in range(DC)]
                if c0 > 0:
                    T.op("dve", lambda e: e.memset(xf[:, :, 0:c0], 0.0), writes=xtok)
                if c0 + nl < n + 2 * HALO:
                    T.op("dve", lambda e: e.memset(xf[:, :, c0 + nl:n + 2 * HALO], 0.0), writes=xtok)
                gcol = V_POOLN + 16 * j
                for c in range(DC):
                    T.op("dve", lambda e, c=c: e.scalar_tensor_tensor(
                        out=xf[:, c, c0:c0 + nl], in0=hb[:, c, c0:c0 + nl], scalar=self.vecs[:, gcol + c:gcol + c + 1],
                        in1=rstd[:, 0:nl], op0=ALU.mult, op1=ALU.mult),
                        reads=[(htok, c), rtok, "vecs"], writes=[("xf", c)])
                if ti + 1 < len(PTILES):
                    prep(ti + 1)
                T.op("dve", lambda e: e.tensor_tensor(out=A[:, :, 1:W], in0=xf[:, :, 0:W - 1], in1=xf[:, :, 1:W], op=ALU.add),
                     reads=xtok, writes=["pA"])
                T.op("dve", lambda e: e.tensor_tensor(out=B[:, :, 2:W - 1], in0=A[:, 4:16, 1:W - 2], in1=A[:, 4:16, 3:W], op=ALU.add),
                     reads=["pA"], writes=["pB"])
                T.op("dve", lambda e: e.tensor_tensor(out=A[:, 8:16, 4:W - 3], in0=B[:, 4:12, 2:W - 5], in1=B[:, 4:12, 6:W - 1], op=ALU.add),
                     reads=["pB"], writes=["pA"])
                T.op("dve", lambda e: e.tensor_tensor(out=B[:, 8:12, 8:W - 7], in0=A[:, 12:16, 4:W - 11], in1=A[:, 12:16, 12:W - 3], op=ALU.add),
                     reads=["pA"], writes=["pB"])
                assert U0 + n <= W - 7
                srcs = [A[:, 0:4, :], B[:, 0:4, :], A[:, 8:12, :], B[:, 8:12, :]]
                stoks = ["pA", "pB", "pA", "pB"]
                for g, w in enumerate(POOL_W):
                    T.op("dve", lambda e, g=g, w=w: e.scalar_tensor_tensor(
                        out=mx[:, 4 * g:4 * g + 4, 0:n], in0=srcs[g][:, :, U0:U0 + n], scalar=1.0 / w,
                        in1=xf[:, 4 * g:4 * g + 4, U0:U0 + n], op0=ALU.mult, op1=ALU.subtract),
                        reads=[stoks[g]] + [("xf", 4 * g + c) for c in range(4)], writes=[("mx", 4 * g + c) for c in range(4)])
                edges = []
                if a == 0:
                    edges.append((0, 0))
                if a + n == L:
                    edges.append((n - HALO, 1))
                for (e0, ei) in edges:
                    T.dma("sp", out=ic[:, :, ei * HALO:(ei + 1) * HALO],
                          in_=self.icnt_d[:, :, a + e0:a + e0 + HALO].rearrange("g p t -> p g t"), writes=[("ic", ei)])
                    for c in range(DC):
                        g = c // 4
                        T.op("dve", lambda e, c=c, g=g, e0=e0, ei=ei: e.tensor_tensor(
                            out=E[:, c, :], in0=srcs[g][:, c % 4, U0 + e0:U0 + e0 + HALO], in1=ic[:, g, ei * HALO:(ei + 1) * HALO], op=ALU.mult),
                            reads=[stoks[g], ("ic", ei)], writes=[("pE", c)])
                    for c in range(DC):
                        T.op("dve", lambda e, c=c, e0=e0: e.tensor_tensor(
                            out=mx[:, c, e0:e0 + HALO], in0=E[:, c, :], in1=xf[:, c, U0 + e0:U0 + e0 + HALO], op=ALU.subtract),
                            reads=[("pE", c), ("xf", c)], writes=[("mx", c)])
                for g in range(4):
                    slot, stok = self.w_get(("pool", j, g))
                    for oc in range(4):
                        d = 4 * g + oc
                        b = self.nb()

                        def mm(e, oc=oc, b=b, slot=slot, g=g):
                            for icc in range(4):
                                ins = e.matmul(self.ps[b][:, 0:n], lhsT=slot[:, icc, oc * 128:(oc + 1) * 128], rhs=mx[:, 4 * g + icc, 0:n],
                                               start=(icc == 0), stop=(icc == 3))
                            return ins
                        T.op("pe", mm, reads=[stok] + [("mx", 4 * g + icc) for icc in range(4)], writes=[("ps", b)])
                        T.op("dve", lambda e, d=d, b=b: e.scalar_tensor_tensor(
                            out=hb[:, d, HALO:HALO + n], in0=self.ps[b][:, 0:n], scalar=self.vecs[:, V_POOLS + 16 * j + d:V_POOLS + 16 * j + d + 1],
                            in1=hb[:, d, HALO:HALO + n], op0=ALU.mult, op1=ALU.add),
                            reads=[("ps", b), (htok, d), "vecs"], writes=[(htok, d)])
                    self.w_done()
                self.store_tile(dst, s, ti, a, n, hb, htok, dtag, col0=HALO, tiles=PTILES)
            T.barrier()


def _rope_tables():
    t = np.arange(SEQ)
    r = (t // GRID_W).astype(np.float32)
    c = (t % GRID_W).astype(np.float32)
    axis_dim = HD // 2
    inv_freq = (10000.0 ** (-np.arange(0, axis_dim, 2, dtype=np.float32) / axis_dim)).astype(np.float32)
    ang = np.concatenate([r[:, None] * inv_freq[None], c[:, None] * inv_freq[None]], axis=-1)
    ang = np.concatenate([np.zeros((NMETA, HD // 2), np.float32), ang], axis=0)
    cos = np.cos(ang).astype(np.float32).T
    sin = np.sin(ang).astype(np.float32).T
    cosT = np.concatenate([cos, cos], axis=0)
    sinS = np.concatenate([-sin, sin], axis=0)
    return np.ascontiguousarray(cosT), np.ascontiguousarray(sinS)


def _inv_counts():
    t = np.arange(L)
    out = np.zeros((4, 128, L), np.float32)
    for g, w in enumerate(POOL_W):
        lo = np.clip(t - w // 2, 0, L)
        hi = np.clip(t - w // 2 + w, 0, L)
        cnt = (hi - lo).astype(np.float32)
        out[g] = (np.float32(1.0) / cnt)[None, :]
    return out


_PERM = np.concatenate([np.arange(0, HD, 2), np.arange(1, HD, 2)])


def _chunk_cols(v):
    return np.ascontiguousarray(np.asarray(v, np.float32).reshape(DC, 128).T)


def _prep_shared(inp):
    w_qkv = np.asarray(inp["w_qkv"], np.float32)
    nq = NH * HD
    colperm = np.arange(3072)
    for h in range(NH):
        colperm[h * HD:(h + 1) * HD] = h * HD + _PERM
    for h in range(NKV):
        colperm[nq + h * HD:nq + (h + 1) * HD] = nq + h * HD + _PERM
    w_qkv_p = np.ascontiguousarray(w_qkv[:, :, colperm])
    vecs = np.zeros((128, NV), np.float32)
    for j in range(2):
        vecs[:, V_ATTN + 16 * j:V_ATTN + 16 * j + 16] = _chunk_cols(inp["attn_norm"][j])
        vecs[:, V_POOLN + 16 * j:V_POOLN + 16 * j + 16] = _chunk_cols(inp["pool_norm"][j])
        vecs[:, V_POOLS + 16 * j:V_POOLS + 16 * j + 16] = _chunk_cols(inp["pool_scale"][j])
        vecs[:, V_QG + j] = np.asarray(inp["q_norm"], np.float32)[j][_PERM]
        vecs[:, V_KG + j] = np.asarray(inp["k_norm"], np.float32)[j][_PERM]
    for i in range(4):
        vecs[:, V_MLP + 16 * i:V_MLP + 16 * i + 16] = _chunk_cols(inp["mlp_norm"][i])
    vecs[:, V_FINAL:V_FINAL + 16] = _chunk_cols(inp["final_norm"])
    grep = np.zeros((4, 128, 128), np.float32)
    for j in range(2):
        grep[j] = np.asarray(inp["q_norm"], np.float32)[j][None, :]
        grep[2 + j] = np.asarray(inp["k_norm"], np.float32)[j][None, :]
    rot = np.zeros((128, 128), np.float32)
    for m in range(128):
        rot[(m + 64) % 128, m] = 1.0
    cosT, sinS = _rope_tables()
    return {
        "w_qkv": w_qkv_p,
        "w_o": np.ascontiguousarray(np.asarray(inp["w_o"], np.float32)),
        "w_pool": np.ascontiguousarray(np.asarray(inp["w_pool"], np.float32).reshape(2, 2048, 512)),
        "w_up": np.ascontiguousarray(np.asarray(inp["w_up"], np.float32)),
        "w_down": np.ascontiguousarray(np.asarray(inp["w_down"], np.float32)),
        "vecs": vecs, "grep": grep, "rot": rot, "cosT": cosT, "sinS": sinS, "icnt": _inv_counts(),
    }


def _prep_h0(xs, meta):
    out = np.empty((len(xs), D, L), np.float32)
    mT = np.asarray(meta, np.float32).T
    for i, x in enumerate(xs):
        out[i, :, :NMETA] = mT
        out[i, :, NMETA:] = np.asarray(x, np.float32).T
    return out


_CACHE = {}


def _get_prog(nseq, nlayers):
    key = (nseq, nlayers)
    if key not in _CACHE:
        p = Prog(nseq, nlayers)
        _CACHE[key] = p.build()
    return _CACHE[key]


def run_sequences(inp, seqs, ncores, nseq, nlayers=4, trace=False):
    shared = _prep_shared(inp)
    nc = _get_prog(nseq, nlayers)
    in_maps = []
    for c in range(ncores):
        m = dict(shared)
        m["h0"] = _prep_h0(seqs[c * nseq:(c + 1) * nseq], inp["meta_tokens"])
        in_maps.append(m)
    res = run_bass_kernel_spmd(nc, in_maps, core_ids=list(range(ncores)), trace=trace)
    outs = []
    for c in range(ncores):
        yT = res.results[c]["yT"]
        for k in range(nseq):
            outs.append(np.ascontiguousarray(yT[k].T))
    return outs, res


def kernel(**inputs):
    xp = np.asarray(inputs["x_prompt"], np.float32)
    xs = np.asarray(inputs["x_sample"], np.float32)
    seqs = [xp[b] for b in range(xp.shape[0])] + [xs[b] for b in range(xs.shape[0])]
    nseq = len(seqs) // NCORES
    outs, _ = run_sequences(inputs, seqs, NCORES, nseq)
    y_prompt = np.stack(outs[:xp.shape[0]], axis=0)
    y_sample = np.stack(outs[xp.shape[0]:], axis=0)
    return (y_prompt, y_sample)
```

```python
import os
from contextlib import ExitStack
import numpy as np
import concourse.bass as bass
import concourse.mybir as mybir
from concourse.bass_utils import run_bass_kernel_spmd

F32 = mybir.dt.float32
BF16 = mybir.dt.bfloat16
ALU = mybir.AluOpType
AF = mybir.ActivationFunctionType

D = 2048
DC = 16
FF = 8192
FC = 64
NH = 16
NKV = 4
HD = 128
NMETA = 16
SEQ = 2048
L = SEQ + NMETA
GRID_W = 64
EPS = 1e-6
NCORES = 8
POOL_W = (2, 4, 8, 16)
HALO = 8

V_ATTN = 0
V_POOLN = 32
V_POOLS = 64
V_MLP = 96
V_FINAL = 160
V_QG = 176
V_KG = 178
NV = 180


def _tiles(n_tiles=5):
    base = L // n_tiles
    rem = L - base * n_tiles
    out = []
    a = 0
    for i in range(n_tiles):
        t = base + (1 if i < rem else 0)
        out.append((a, t))
        a += t
    return out


TILES = _tiles(5)
PTILES = _tiles(6)
PTM = max(t for _, t in PTILES)
TM = max(t for _, t in TILES)
KVW = 256
KV_TILES = [(i * KVW, KVW) for i in range(SEQ // KVW)] + [(2048, 16)]
KCHUNKS = [(i * 128, 128) for i in range(16)] + [(2048, 16)]


class Ev:
    __slots__ = ("key", "sem", "val", "eng")

    def __init__(self, key, sem, val, eng):
        self.key = key
        self.sem = sem
        self.val = val
        self.eng = eng


class Tracker:
    def __init__(self, nc, es, n_dma_sems=20):
        self.nc = nc
        self.eng = {"pe": nc.tensor, "act": nc.scalar, "dve": nc.vector, "pool": nc.gpsimd, "sp": nc.sync}
        self.csem = {k: es.enter_context(nc.semaphore("s_" + k)) for k in ("pe", "act", "dve", "pool")}
        self.cnt = {k: 0 for k in self.csem}
        self.seen = {k: {} for k in self.eng}
        self.dsem = {}
        self.dnext = {}
        for q in ("sp", "pool"):
            self.dsem[q] = [es.enter_context(nc.semaphore("d_%s%d" % (q, i))) for i in range(n_dma_sems)]
            self.dnext[q] = 0
        self.dcnt = {}
        self.tok = {}

    def _st(self, t):
        s = self.tok.get(t)
        if s is None:
            s = [{}, {}]
            self.tok[t] = s
        return s

    def _wait(self, e, ev):
        if self.seen[e].get(ev.key, 0) >= ev.val:
            return
        self.eng[e].wait_ge(ev.sem, ev.val)
        self.seen[e][ev.key] = ev.val

    def _deps(self, e, reads, writes, acc):
        for r in reads:
            for ev in self._st(r)[0].values():
                if ev.eng == e:
                    if e != "pe":
                        self._wait(e, ev)
                else:
                    self._wait(e, ev)
        for w in writes:
            st = self._st(w)
            evs = list(st[1].values())
            if not acc:
                evs += list(st[0].values())
            for ev in evs:
                if ev.eng == e:
                    continue
                self._wait(e, ev)

    def _commit(self, ev, reads, writes, acc):
        for w in writes:
            st = self._st(w)
            if acc and not st[1]:
                st[0][ev.key] = ev
            else:
                st[0] = {ev.key: ev}
                st[1] = {}
        for r in reads:
            if r in writes:
                continue
            self._st(r)[1][ev.key] = ev

    def op(self, e, fn, reads=(), writes=()):
        self._deps(e, reads, writes, False)
        ins = fn(self.eng[e])
        self.cnt[e] += 1
        ins.then_inc(self.csem[e], 1)
        ev = Ev(e, self.csem[e], self.cnt[e], e)
        self._commit(ev, reads, writes, False)
        return ev

    def dma(self, q, out, in_, reads=(), writes=(), acc=False, **kw):
        self._deps(q, reads, writes, acc)
        i = self.dnext[q]
        self.dnext[q] = (i + 1) % len(self.dsem[q])
        key = "d_%s%d" % (q, i)
        sem = self.dsem[q][i]
        prev = self.dcnt.get(key, 0)
        if prev:
            self._wait(q, Ev(key, sem, prev, None))
        ins = self.eng[q].dma_start(out=out, in_=in_, **kw)
        ins.then_inc(sem, 16)
        self.dcnt[key] = prev + 16
        ev = Ev(key, sem, prev + 16, None)
        self._commit(ev, reads, writes, acc)
        return ev

    def barrier(self, engines=("pe", "act", "dve", "pool", "sp")):
        evs = [Ev(k, self.csem[k], self.cnt[k], k) for k in self.csem if self.cnt[k] > 0]
        for q in ("sp",):
            for i, sem in enumerate(self.dsem[q]):
                key = "d_%s%d" % (q, i)
                if self.dcnt.get(key, 0):
                    evs.append(Ev(key, sem, self.dcnt[key], None))
        for e in engines:
            for ev in evs:
                if ev.eng == e:
                    continue
                self._wait(e, ev)
        self.tok = {k: v for k, v in self.tok.items() if isinstance(k, tuple) and k[0] == "wbf"}


class Prog:
    def __init__(self, nseq, nlayers, nslots=3):
        self.nseq = nseq
        self.nlayers = nlayers
        self.nslots = nslots
        self.uid = 0
        self.tiling = {}

    def sb(self, es, shape, dtype, name="t"):
        self.uid += 1
        return es.enter_context(self.nc.sbuf_tensor("%s_%d" % (name, self.uid), list(shape), dtype))

    def nb(self):
        b = self.bank_rr[self.bank_i % len(self.bank_rr)]
        self.bank_i += 1
        return b

    def plan_weights(self):
        sched = []
        for s in range(self.nseq):
            for i in range(self.nlayers):
                j = i // 2
                if i % 2 == 0:
                    for _ in KV_TILES:
                        sched.append(("qkv", j, 4))
                        sched.append(("qkv", j, 5))
                    for _ in TILES:
                        for b in range(4):
                            sched.append(("qkv", j, b))
                        for b in range(4):
                            sched.append(("wo", j, b))
                else:
                    for _ in PTILES:
                        for g in range(4):
                            sched.append(("pool", j, g))
                for _ in TILES:
                    for fb in range(16):
                        sched.append(("up", i, fb))
                    for jq in range(4):
                        for fb4 in range(4):
                            sched.append(("down", i, jq, fb4))
        self.sched = sched
        self.w_next_load = 0
        self.w_cur = -1

    def _w_src(self, key):
        kind = key[0]
        if kind == "qkv":
            _, j, b = key
            return (self.wb["qkv"][j][:, b * 512:(b + 1) * 512].rearrange("(kc p) n -> p kc n", p=128), 16,
                    ("wbf", "qkv", j))
        if kind == "wo":
            _, j, b = key
            return (self.wb["wo"][j][:, b * 512:(b + 1) * 512].rearrange("(kc p) n -> p kc n", p=128), 16,
                    ("wbf", "wo", j))
        if kind == "pool":
            _, j, g = key
            return (self.wb["pool"][j][g * 512:(g + 1) * 512, :].rearrange("(kc p) n -> p kc n", p=128), 4,
                    ("wbf", "pool", j))
        if kind == "up":
            _, i, fb = key
            return (self.wb["up"][i][:, fb * 512:(fb + 1) * 512].rearrange("(kc p) n -> p kc n", p=128), 16,
                    ("wbf", "up", i))
        _, i, jq, fb4 = key
        return (self.wb["down"][i][fb4 * 2048:(fb4 + 1) * 2048, jq * 512:(jq + 1) * 512]
                .rearrange("(kc p) n -> p kc n", p=128), 16, ("wbf", "down", i))

    def _w_emit_load(self):
        i = self.w_next_load
        if i >= len(self.sched):
            return
        src, nk, tok = self._w_src(self.sched[i])
        sl = i % self.nslots
        self.T.dma("sp", out=self.wslot[sl][:, 0:nk, :], in_=src, reads=[tok], writes=[("ws", sl)])
        self.w_next_load += 1

    def w_get(self, key):
        self.w_cur += 1
        assert self.sched[self.w_cur] == key, (self.sched[self.w_cur], key)
        while self.w_next_load <= self.w_cur:
            self._w_emit_load()
        sl = self.w_cur % self.nslots
        return self.wslot[sl], ("ws", sl)

    def w_done(self):
        while self.w_next_load < min(len(self.sched), self.w_cur + self.nslots + 1):
            self._w_emit_load()

    def cast_w(self, kind, j):
        src = self.w32[kind][j]
        dst = self.wb[kind][j]
        rows, cols = src.shape
        step = max(1, (4 * 1024 * 1024) // cols)
        for r0 in range(0, rows, step):
            r1 = min(rows, r0 + step)
            self.T.dma("pool", out=dst[r0:r1, :], in_=src[r0:r1, :], writes=[("wbf", kind, j)], acc=True,
                       max_dma_last_dim=2048)

    def lazy_casts(self, s, phase, j):
        if s != 0:
            return
        nl = self.nlayers
        todo = []
        if phase == "q":
            i = 2 * j
            todo += [("up", i), ("down", i)]
            if i + 1 < nl:
                todo += [("pool", j), ("up", i + 1), ("down", i + 1)]
        elif phase == "pool":
            if 2 * j + 2 < nl:
                todo += [("qkv", j + 1), ("wo", j + 1)]
        for kind, idx in todo:
            self.cast_w(kind, idx)

    def build(self):
        nc = bass.Bass("TRN2", target_bir_lowering=False)
        self.nc = nc
        nseq = self.nseq
        dt = nc.dram_tensor
        self.h0 = dt("h0", [nseq, D, L], F32, kind="ExternalInput").ap()
        self.yT = dt("yT", [nseq, D, SEQ], F32, kind="ExternalOutput").ap()
        w32 = {
            "qkv": dt("w_qkv", [2, D, 3072], F32, kind="ExternalInput").ap(),
            "wo": dt("w_o", [2, D, D], F32, kind="ExternalInput").ap(),
            "pool": dt("w_pool", [2, 2048, 512], F32, kind="ExternalInput").ap(),
            "up": dt("w_up", [4, D, FF], F32, kind="ExternalInput").ap(),
            "down": dt("w_down", [4, FF, D], F32, kind="ExternalInput").ap(),
        }
        self.wb = {
            "qkv": dt("wb_qkv", [2, D, 3072], BF16, kind="Internal").ap(),
            "wo": dt("wb_o", [2, D, D], BF16, kind="Internal").ap(),
            "pool": dt("wb_pool", [2, 2048, 512], BF16, kind="Internal").ap(),
            "up": dt("wb_up", [4, D, FF], BF16, kind="Internal").ap(),
            "down": dt("wb_down", [4, FF, D], BF16, kind="Internal").ap(),
        }
        vecs_d = dt("vecs", [128, NV], F32, kind="ExternalInput").ap()
        grep_d = dt("grep", [4, 128, 128], F32, kind="ExternalInput").ap()
        rot_d = dt("rot", [128, 128], F32, kind="ExternalInput").ap()
        cos_d = dt("cosT", [128, L], F32, kind="ExternalInput").ap()
        sin_d = dt("sinS", [128, L], F32, kind="ExternalInput").ap()
        self.cos_d, self.sin_d = cos_d, sin_d
        self.icnt_d = dt("icnt", [4, 128, L], F32, kind="ExternalInput").ap()
        self.hbuf = [dt("hA", [nseq, D, L], F32, kind="Internal").ap(),
                     dt("hB", [nseq, D, L], F32, kind="Internal").ap()]

        with ExitStack() as es:
            T = Tracker(nc, es)
            self.T = T
            self.pst = es.enter_context(nc.psum_tensor("pst", [128, 8, 512], F32))
            self.ps = [self.pst[:, i, :] for i in range(8)]
            self.bank_rr = list(range(8))
            self.bank_i = 0
            self.wslot = [self.sb(es, [128, 16, 512], BF16, "ws") for _ in range(self.nslots)]
            self.vecs = self.sb(es, [128, NV], F32, "vecs")
            self.rot = self.sb(es, [128, 128], F32, "rot")
            self.onesD = self.sb(es, [128, 128], BF16, "onesD")
            self.onesH = self.sb(es, [128, 128], BF16, "onesH")
            self.ones1 = self.sb(es, [128, 128], BF16, "ones1")
            self.epst = self.sb(es, [128, 1], F32, "eps")
            self.negb = self.sb(es, [128, 2], F32, "negb")
            gtmp = self.sb(es, [128, 4, 128], F32, "gtmp")
            gmax = self.sb(es, [128, 4], F32, "gmax")

            self.w32 = w32
            self.cast_w("qkv", 0)
            self.cast_w("wo", 0)
            T.dma("sp", out=self.vecs[:], in_=vecs_d[:, :], writes=["vecs"])
            T.dma("sp", out=self.rot[:], in_=rot_d[:, :], writes=["rot"])
            T.dma("sp", out=gtmp[:], in_=grep_d.rearrange("g p n -> p g n"), writes=["gtmp"])
            T.op("dve", lambda e: e.memset(self.onesD[:], 1.0 / D), writes=["onesD"])
            T.op("dve", lambda e: e.memset(self.onesH[:], 1.0 / HD), writes=["onesH"])
            T.op("dve", lambda e: e.memset(self.ones1[:], 1.0), writes=["ones1"])
            T.op("dve", lambda e: e.memset(self.epst[:], EPS), writes=["eps"])
            T.op("dve", lambda e: e.tensor_reduce(out=gmax[:], in_=gtmp[:], axis=mybir.AxisListType.X, op=ALU.max, apply_absolute_value=True),
                 reads=["gtmp"], writes=["gmax"])
            for j in range(2):
                T.op("dve", lambda e, j=j: e.scalar_tensor_tensor(
                    out=self.negb[:, j:j + 1], in0=gmax[:, j:j + 1], scalar=-float(np.sqrt(HD)),
                    in1=gmax[:, 2 + j:3 + j], op0=ALU.mult, op1=ALU.mult), reads=["gmax"], writes=["negb"])
            T.barrier()

            self.plan_weights()
            for s in range(nseq):
                sl = 0
                for i in range(self.nlayers):
                    j = i // 2
                    src = self.h0 if sl == 0 else self.hbuf[(sl - 1) % 2]
                    dst = self.hbuf[sl % 2]
                    stag = ("h0",) if sl == 0 else ("hb", (sl - 1) % 2)
                    dtag = ("hb", sl % 2)
                    if i % 2 == 0:
                        self.attn_layer(s, j, src, dst, stag, dtag)
                    else:
                        self.pool_layer(s, j, src, dst, stag, dtag)
                    sl += 1
                    src = self.hbuf[(sl - 1) % 2]
                    dst = self.hbuf[sl % 2]
                    stag = ("hb", (sl - 1) % 2)
                    dtag = ("hb", sl % 2)
                    self.mlp_layer(s, i, src, dst, stag, dtag, final=(i == self.nlayers - 1))
                    sl += 1
            T.barrier()
        return nc

    def load_tile(self, src, s, a, n, hb, htok, stag, col0=0, ngroups=1):
        T = self.T
        wt = self.tiling.get(stag + (s,), TILES)
        rd = [stag + (s, ti) for ti, (ta, tt) in enumerate(wt) if ta < a + n and ta + tt > a]
        gs = DC // ngroups
        for g in range(ngroups):
            T.dma("sp", out=hb[:, g * gs:(g + 1) * gs, col0:col0 + n],
                  in_=src[s][g * gs * 128:(g + 1) * gs * 128, a:a + n].rearrange("(c p) t -> p c t", p=128),
                  reads=rd, writes=[(htok, c) for c in range(g * gs, (g + 1) * gs)])

    def rmsnorm_a(self, hb, htok, n, sq, col0=0, sqtoks=("sq",)):
        hr = [(htok, c) for c in range(DC)]
        self.T.op("act", lambda e: e.activation(out=sq[:, :, 0:n], in_=hb[:, :, col0:col0 + n], func=AF.Square),
                  reads=hr, writes=list(sqtoks))

    def rmsnorm_b(self, hb, htok, n, gcol, sq, rstd, out, otok, col0=0, sqtoks=("sq",), engs=("dve",), rtok="rstd"):
        T = self.T
        b = self.nb()
        ps = self.ps[b]

        def mm(e):
            for c in range(DC):
                ins = e.matmul(ps[:, 0:n], lhsT=self.onesD[:, :], rhs=sq[:, c, 0:n], start=(c == 0), stop=(c == DC - 1))
            return ins
        T.op("pe", mm, reads=list(sqtoks) + ["onesD"], writes=[("ps", b)])
        T.op("act", lambda e: e.activation(out=rstd[:, 0:n], in_=ps[:, 0:n], func=AF.Ln, bias=self.epst[:, 0:1], scale=1.0),
             reads=[("ps", b), "eps"], writes=[rtok])
        T.op("act", lambda e: e.activation(out=rstd[:, 0:n], in_=rstd[:, 0:n], func=AF.Exp, scale=-0.5), reads=[rtok], writes=[rtok])
        for c in range(DC):
            T.op(engs[c % len(engs)], lambda e, c=c: e.scalar_tensor_tensor(
                out=out[:, c, 0:n], in0=hb[:, c, col0:col0 + n], scalar=self.vecs[:, gcol + c:gcol + c + 1],
                in1=rstd[:, 0:n], op0=ALU.mult, op1=ALU.mult),
                reads=[(htok, c), rtok, "vecs"], writes=[(otok, c)])

    def rmsnorm(self, hb, htok, n, gcol, sq, rstd, out, otok, col0=0, sqtoks=("sq",), engs=("dve",), rtok="rstd"):
        self.rmsnorm_a(hb, htok, n, sq, col0, sqtoks)
        self.rmsnorm_b(hb, htok, n, gcol, sq, rstd, out, otok, col0, sqtoks, engs, rtok)

    def store_tile(self, dst, s, ti, a, n, hb, htok, dtag, col0=0, tiles=None, ngroups=1):
        self.tiling[dtag + (s,)] = TILES if tiles is None else tiles
        gs = DC // ngroups
        for g in range(ngroups):
            self.T.dma("sp", out=dst[s][g * gs * 128:(g + 1) * gs * 128, a:a + n].rearrange("(c p) t -> p c t", p=128),
                       in_=hb[:, g * gs:(g + 1) * gs, col0:col0 + n],
                       reads=[(htok, c) for c in range(g * gs, (g + 1) * gs)], writes=[dtag + (s, ti)], acc=(ngroups > 1))

    def store_tile_group(self, dst, s, ti, a, n, hb, htok, dtag, g, gs=4):
        self.tiling[dtag + (s,)] = TILES
        self.T.dma("sp", out=dst[s][g * gs * 128:(g + 1) * gs * 128, a:a + n].rearrange("(c p) t -> p c t", p=128),
                   in_=hb[:, g * gs:(g + 1) * gs, 0:n],
                   reads=[(htok, c) for c in range(g * gs, (g + 1) * gs)], writes=[dtag + (s, ti)], acc=True)

    def mlp_layer(self, s, i, src, dst, stag, dtag, final):
        T = self.T
        with ExitStack() as es:
            hT = [self.sb(es, [128, DC, TM], F32, "hT") for _ in range(2)]
            sq = self.sb(es, [128, DC, TM], BF16, "sq")
            hn = self.sb(es, [128, DC, TM], BF16, "hn")
            rstd = self.sb(es, [128, TM], F32, "rstd")
            rl = [self.sb(es, [128, TM], F32, "rl") for _ in range(3)]
            uT = self.sb(es, [128, FC, TM], BF16, "uT")
            if final:
                sq2 = uT[:, 32:48, :]
                sq2toks = [("uT", f) for f in range(32, 48)]
                rstd2 = self.sb(es, [128, TM], F32, "rstd2")
            def prep_a(ti):
                a, n = TILES[ti]
                self.load_tile(src, s, a, n, hT[ti % 2], "hT%d" % (ti % 2), stag)
                self.rmsnorm_a(hT[ti % 2], "hT%d" % (ti % 2), n, sq)

            def prep_b(ti):
                a, n = TILES[ti]
                self.rmsnorm_b(hT[ti % 2], "hT%d" % (ti % 2), n, V_MLP + 16 * i, sq, rstd, hn, "hn")

            prep_a(0)
            prep_b(0)
            for ti, (a, n) in enumerate(TILES):
                hb = hT[ti % 2]
                htok = "hT%d" % (ti % 2)
                hnr = [("hn", c) for c in range(DC)]
                for fb in range(16):
                    slot, stok = self.w_get(("up", i, fb))
                    for fc in range(4):
                        f = fb * 4 + fc
                        b = self.nb()
                        ps = self.ps[b]

                        def mm(e, fc=fc, ps=ps, slot=slot):
                            for kc in range(DC):
                                ins = e.matmul(ps[:, 0:n], lhsT=slot[:, kc, fc * 128:(fc + 1) * 128], rhs=hn[:, kc, 0:n],
                                               start=(kc == 0), stop=(kc == DC - 1))
                            return ins
                        T.op("pe", mm, reads=[stok] + hnr, writes=[("ps", b)])
                        r = rl[f % 3]
                        T.op("act", lambda e, r=r, ps=ps: e.activation(out=r[:, 0:n], in_=ps[:, 0:n], func=AF.Relu),
                             reads=[("ps", b)], writes=[("rl", f % 3)])
                        T.op("dve", lambda e, r=r, f=f: e.tensor_tensor(out=uT[:, f, 0:n], in0=r[:, 0:n], in1=r[:, 0:n], op=ALU.mult),
                             reads=[("rl", f % 3)], writes=[("uT", f)])
                    self.w_done()
                if ti + 1 < len(TILES):
                    prep_a(ti + 1)
                for jq in range(4):
                    if jq == 1 and ti + 1 < len(TILES):
                        prep_b(ti + 1)
                    banks = [4 * (jq % 2) + dd for dd in range(4)]
                    for fb4 in range(4):
                        slot, stok = self.w_get(("down", i, jq, fb4))

                        def mm(e, slot=slot, fb4=fb4, banks=banks):
                            for fl in range(16):
                                f = fb4 * 16 + fl
                                for dd in range(4):
                                    ins = e.matmul(self.ps[banks[dd]][:, 0:n], lhsT=slot[:, fl, dd * 128:(dd + 1) * 128],
                                                   rhs=uT[:, f, 0:n], start=(f == 0), stop=(f == FC - 1))
                            return ins
                        T.op("pe", mm, reads=[stok] + [("uT", fb4 * 16 + fl) for fl in range(16)],
                             writes=[("ps", b) for b in banks])
                        self.w_done()
                    for dd in range(4):
                        c = jq * 4 + dd
                        b = banks[dd]
                        T.op("dve", lambda e, c=c, b=b: e.tensor_tensor(out=hb[:, c, 0:n], in0=self.ps[b][:, 0:n], in1=hb[:, c, 0:n], op=ALU.add),
                             reads=[("ps", b), (htok, c)], writes=[(htok, c)])
                if final:
                    self.rmsnorm(hb, htok, n, V_FINAL, sq2, rstd2, hb, htok, sqtoks=sq2toks, rtok="rstd2")
                    lo = max(a, NMETA)
                    T.dma("sp", out=self.yT[s][:, lo - NMETA:a + n - NMETA].rearrange("(c p) t -> p c t", p=128),
                          in_=hb[:, :, lo - a:n], reads=[(htok, c) for c in range(DC)], writes=[("y", s, ti)])
                else:
                    self.store_tile(dst, s, ti, a, n, hb, htok, dtag)
            T.barrier()

    def qk_pipeline(self, groups, n, a_off, cs, sn, sets, fillers=(), peng="pool", cstoks=("cs", "sn")):
        T = self.T
        G = len(groups)
        PA = [(0, 1), (2, 3)]
        SB = [(4, 5), (6, 7)]
        fillers = list(fillers)
        ns = len(sets)
        for it in range(G + 2):
            deferred = None
            g = it
            if g < G:
                pa = PA[g % 2]
                k = g % ns
                xraw, sqh, rsh, xn = sets[k]
                groups[g]["proj"](pa)
                pv = self.pst[:, pa[0]:pa[0] + 2, 0:n]
                prd = [("ps", pa[0]), ("ps", pa[1])]
                T.op("act", lambda e: e.copy(out=xraw[:, :, 0:n], in_=pv), reads=prd, writes=[("xraw", k)])
                T.op("act", lambda e: e.activation(out=sqh[:, :, 0:n], in_=pv, func=AF.Square), reads=prd, writes=[("sqh", k)])
            else:
                for _ in range(2):
                    if fillers:
                        fillers.pop(0)()
            g = it - 1
            if 0 <= g < G:
                sbk = SB[g % 2]
                k = g % ns
                xraw, sqh, rsh, xn = sets[k]
                gcol = groups[g]["gcol"]

                def st(e, sbk=sbk, sqh=sqh):
                    for i in range(2):
                        ins = e.matmul(self.ps[sbk[i]][:, 0:n], lhsT=self.onesH[:, :], rhs=sqh[:, i, 0:n], start=True, stop=True)
                    return ins
                T.op("pe", st, reads=[("sqh", k), "onesH"], writes=[("ps", sbk[0]), ("ps", sbk[1])])
                sv = self.pst[:, sbk[0]:sbk[0] + 2, 0:n]
                T.op("act", lambda e: e.activation(out=rsh[:, :, 0:n], in_=sv, func=AF.Ln, bias=self.epst[:, 0:1], scale=1.0),
                     reads=[("ps", sbk[0]), ("ps", sbk[1]), "eps"], writes=[("rsh", k)])
                T.op("act", lambda e: e.activation(out=rsh[:, :, 0:n], in_=rsh[:, :, 0:n], func=AF.Exp, scale=-0.5),
                     reads=[("rsh", k)], writes=[("rsh", k)])
                deferred = (xraw, rsh, xn, gcol, k)
            g = it - 2
            if 0 <= g < G:
                sbk = SB[g % 2]
                k = g % ns
                xraw, sqh, rsh, xn = sets[k]

                def rt(e, sbk=sbk, xn=xn):
                    for i in range(2):
                        ins = e.matmul(self.ps[sbk[i]][:, 0:n], lhsT=self.rot[:, :], rhs=xn[:, i, 0:n], start=True, stop=True)
                    return ins
                T.op("pe", rt, reads=[("xn", k), "rot"], writes=[("ps", sbk[0]), ("ps", sbk[1])])
                for i in range(2):
                    T.op(peng, lambda e, i=i: e.tensor_tensor(out=xraw[:, i, 0:n], in0=xn[:, i, 0:n], in1=cs[:, 0:n], op=ALU.mult),
                         reads=[("xn", k), cstoks[0]], writes=[("xraw", k)])
                for i in range(2):
                    T.op("dve", lambda e, i=i: e.tensor_tensor(out=rsh[:, i, 0:n], in0=self.ps[sbk[i]][:, 0:n], in1=sn[:, 0:n], op=ALU.mult),
                         reads=[("ps", sbk[i]), cstoks[1]], writes=[("rsh", k)])
                for i in range(2):
                    oap, otok = groups[g]["outs"][i]
                    T.op(peng, lambda e, i=i, oap=oap: e.tensor_tensor(out=oap, in0=xraw[:, i, 0:n], in1=rsh[:, i, 0:n], op=ALU.add),
                         reads=[("xraw", k), ("rsh", k)], writes=[otok])
            if deferred is not None:
                xraw, rsh, xn, gcol, k = deferred
                T.op("dve", lambda e: e.scalar_tensor_tensor(out=xn[:, :, 0:n], in0=xraw[:, :, 0:n], scalar=self.vecs[:, gcol:gcol + 1],
                                                             in1=rsh[:, :, 0:n], op0=ALU.mult, op1=ALU.mult),
                     reads=[("xraw", k), ("rsh", k), "vecs"], writes=[("xn", k)])
        while fillers:
            fillers.pop(0)()

    def attn_layer(self, s, j, src, dst, stag, dtag):
        T = self.T
        scale = float(HD ** -0.5)
        with ExitStack() as es:
            kT = self.sb(es, [128, NKV, L], BF16, "kT")
            vS = self.sb(es, [128, 17, 512], BF16, "vS")
            cs = self.sb(es, [128, 512], F32, "cs")
            sn = self.sb(es, [128, 512], F32, "sn")
            rstd_all = self.sb(es, [128, L], F32, "rstda")

            def load_cs(a, n):
                T.dma("sp", out=cs[:, 0:n], in_=self.cos_d[:, a:a + n], writes=["cs"])
                T.dma("sp", out=sn[:, 0:n], in_=self.sin_d[:, a:a + n], writes=["sn"])
            with ExitStack() as es2:
                hbs = [self.sb(es2, [128, DC, KVW], F32, "hTk") for _ in range(2)]
                sq = self.sb(es2, [128, DC, KVW], BF16, "sqk")
                hns = [self.sb(es2, [128, DC, KVW], BF16, "hnk") for _ in range(2)]
                css = [(self.sb(es2, [128, KVW], F32, "csk"), self.sb(es2, [128, KVW], F32, "snk")) for _ in range(2)]
                sets = [(self.sb(es2, [128, 2, KVW], F32, "xraw"), self.sb(es2, [128, 2, KVW], BF16, "sqh"),
                         self.sb(es2, [128, 2, KVW], F32, "rsh"), self.sb(es2, [128, 2, KVW], F32, "xn")) for _ in range(3)]
                peng = "dve"

                def kv_prep_a(ti):
                    a, n = KV_TILES[ti]
                    p = ti % 2
                    self.load_tile(src, s, a, n, hbs[p], "hT%d" % p, stag)
                    T.dma("sp", out=css[p][0][:, 0:n], in_=self.cos_d[:, a:a + n], writes=["cs%d" % p])
                    T.dma("sp", out=css[p][1][:, 0:n], in_=self.sin_d[:, a:a + n], writes=["sn%d" % p])
                    self.rmsnorm_a(hbs[p], "hT%d" % p, n, sq)

                def kv_prep_b(ti):
                    a, n = KV_TILES[ti]
                    p = ti % 2
                    self.rmsnorm_b(hbs[p], "hT%d" % p, n, V_ATTN + 16 * j, sq, rstd_all[:, a:a + n], hns[p], "hn%d" % p,
                                   rtok=("rstda", ti))

                kv_prep_a(0)
                kv_prep_b(0)
                for ti, (a, n) in enumerate(KV_TILES):
                    p = ti % 2
                    hn = hns[p]
                    hnr = [("hn%d" % p, c) for c in range(DC)]
                    wst = {}
                    if ti + 1 < len(KV_TILES):
                        kv_prep_a(ti + 1)

                    def kproj(pa, g, n=n, wst=wst, hnr=hnr, hn=hn):
                        if g == 0:
                            wst["k"] = self.w_get(("qkv", j, 4))
                        slot, stok = wst["k"]
                        for i in range(2):
                            kvh = 2 * g + i
                            b = pa[i]

                            def mm(e, kvh=kvh, b=b):
                                for kc in range(DC):
                                    ins = e.matmul(self.ps[b][:, 0:n], lhsT=slot[:, kc, kvh * 128:(kvh + 1) * 128], rhs=hn[:, kc, 0:n],
                                                   start=(kc == 0), stop=(kc == DC - 1))
                                return ins
                            T.op("pe", mm, reads=[stok] + hnr, writes=[("ps", b)])
                        if g == 1:
                            self.w_done()

                    groups = [dict(proj=(lambda pa, g=g: kproj(pa, g)), gcol=V_KG + j,
                                   outs=[(kT[:, 2 * g + i, a:a + n], ("kT", 2 * g + i, ti)) for i in range(2)]) for g in range(2)]
                    nm = (n + 127) // 128
                    fills = []
                    for m in range(nm):
                        def vfill(m=m, n=n, a=a, nm=nm, wst=wst, hnr=hnr, hn=hn):
                            if m == 0:
                                wst["v"] = self.w_get(("qkv", j, 5))
                            slot, stok = wst["v"]
                            ms = min(128, n - m * 128)
                            b = self.nb()

                            def mm(e):
                                for kc in range(DC):
                                    ins = e.matmul(self.ps[b][0:ms, 0:512], lhsT=hn[:, kc, m * 128:m * 128 + ms], rhs=slot[:, kc, :],
                                                   start=(kc == 0), stop=(kc == DC - 1))
                                return ins
                            T.op("pe", mm, reads=[stok] + hnr, writes=[("ps", b)])
                            kc_i = a // 128 + m
                            T.op("act", lambda e: e.copy(out=vS[0:ms, kc_i, :], in_=self.ps[b][0:ms, 0:512]),
                                 reads=[("ps", b)], writes=[("vS", kc_i)])
                            if m == nm - 1:
                                self.w_done()
                        fills.append(vfill)
                    if ti + 1 < len(KV_TILES):
                        fills.append(lambda ti=ti: kv_prep_b(ti + 1))
                    self.qk_pipeline(groups, n, a, css[p][0], css[p][1], sets, fills, peng=peng, cstoks=("cs%d" % p, "sn%d" % p))
                T.barrier()
            self.lazy_casts(s, "q", j)
            with ExitStack() as es2:
                hb = self.sb(es2, [128, DC, TM], F32, "hTq")
                hn = self.sb(es2, [128, DC, TM], BF16, "hnq")
                qT = self.sb(es2, [128, NH, TM], BF16, "qT")
                oT = self.sb(es2, [128, NH, TM], BF16, "oT")
                P = [self.sb(es2, [128, TM], BF16, "P") for _ in range(4)]
                rcp = [self.sb(es2, [128, TM], F32, "rcp") for _ in range(2)]
                sets = [(self.sb(es2, [128, 2, TM], F32, "xraw"), self.sb(es2, [128, 2, TM], BF16, "sqh"),
                         self.sb(es2, [128, 2, TM], F32, "rsh"), self.sb(es2, [128, 2, TM], F32, "xn")) for _ in range(3)]
                htok = "hTq"
                otoks = [("oT", h) for h in range(NH)]
                for ti, (a, n) in enumerate(TILES):
                    self.load_tile(src, s, a, n, hb, htok, stag, ngroups=4)
                    load_cs(a, n)
                    gcol = V_ATTN + 16 * j
                    for c in range(DC):
                        T.op("dve", lambda e, c=c: e.scalar_tensor_tensor(
                            out=hn[:, c, 0:n], in0=hb[:, c, 0:n], scalar=self.vecs[:, gcol + c:gcol + c + 1],
                            in1=rstd_all[:, a:a + n], op0=ALU.mult, op1=ALU.mult),
                            reads=[(htok, c), "vecs"], writes=[("hn", c)])
                    hnr = [("hn", c) for c in range(DC)]
                    wst = {}

                    def qproj(pa, g, n=n, wst=wst, hnr=hnr):
                        if g % 2 == 0:
                            wst["q"] = self.w_get(("qkv", j, g // 2))
                        slot, stok = wst["q"]
                        for i in range(2):
                            hh = (2 * g + i) % 4
                            b = pa[i]

                            def mm(e, hh=hh, b=b):
                                for kc in range(DC):
                                    ins = e.matmul(self.ps[b][:, 0:n], lhsT=slot[:, kc, hh * 128:(hh + 1) * 128], rhs=hn[:, kc, 0:n],
                                                   start=(kc == 0), stop=(kc == DC - 1))
                                return ins
                            T.op("pe", mm, reads=[stok] + hnr, writes=[("ps", b)])
                        if g % 2 == 1:
                            self.w_done()

                    groups = [dict(proj=(lambda pa, g=g: qproj(pa, g)), gcol=V_QG + j,
                                   outs=[(qT[:, 2 * g + i, 0:n], ("qT", 2 * g + i)) for i in range(2)]) for g in range(NH // 2)]
                    self.qk_pipeline(groups, n, a, cs, sn, sets, peng="dve")
                    steps = [(h, kc) for h in range(NH) for kc in range(len(KCHUNKS))]
                    sbank = {}

                    def S(idx):
                        h, kc = steps[idx]
                        kvh = h // 4
                        k0, ks = KCHUNKS[kc]
                        b = idx % 3
                        sbank[idx] = b
                        T.op("pe", lambda e: e.matmul(self.ps[b][0:ks, 0:n], lhsT=kT[:, kvh, k0:k0 + ks], rhs=qT[:, h, 0:n], start=True, stop=True),
                             reads=[("qT", h)], writes=[("ps", b)])

                    S(0)
                    S(1)
                    for idx, (h, kc) in enumerate(steps):
                        kvh = h // 4
                        k0, ks = KCHUNKS[kc]
                        b = sbank[idx]
                        pi = idx % 4
                        ob = 3 + (h % 2)
                        sb_ = 5 + (h % 2)
                        T.op("act", lambda e, b=b, pi=pi, ks=ks: e.activation(out=P[pi][0:ks, 0:n], in_=self.ps[b][0:ks, 0:n], func=AF.Exp,
                                                                             bias=self.negb[0:ks, j:j + 1], scale=scale),
                             reads=[("ps", b), "negb"], writes=[("P", pi)])
                        if idx + 2 < len(steps):
                            S(idx + 2)
                        last = (kc == len(KCHUNKS) - 1)

                        def pv(e, pi=pi, ks=ks, kc=kc, kvh=kvh, ob=ob, sb_=sb_, last=last):
                            e.matmul(self.ps[ob][:, 0:n], lhsT=vS[0:ks, kc, kvh * 128:(kvh + 1) * 128], rhs=P[pi][0:ks, 0:n],
                                     start=(kc == 0), stop=last)
                            return e.matmul(self.ps[sb_][:, 0:n], lhsT=self.ones1[0:ks, :], rhs=P[pi][0:ks, 0:n],
                                            start=(kc == 0), stop=last)
                        T.op("pe", pv, reads=[("P", pi), "ones1"], writes=[("ps", ob), ("ps", sb_)])
                        if last:
                            rc = rcp[h % 2]
                            T.op("dve", lambda e, rc=rc, sb_=sb_: e.reciprocal(out=rc[:, 0:n], in_=self.ps[sb_][:, 0:n]),
                                 reads=[("ps", sb_)], writes=[("rcp", h % 2)])
                            T.op("dve", lambda e, rc=rc, ob=ob, h=h: e.tensor_tensor(out=oT[:, h, 0:n], in0=self.ps[ob][:, 0:n], in1=rc[:, 0:n], op=ALU.mult),
                                 reads=[("ps", ob), ("rcp", h % 2)], writes=[("oT", h)])
                    for ob_ in range(4):
                        slot, stok = self.w_get(("wo", j, ob_))
                        for dd in range(4):
                            d = ob_ * 4 + dd
                            b = self.nb()

                            def mm(e, dd=dd, b=b, slot=slot):
                                for c in range(NH):
                                    ins = e.matmul(self.ps[b][:, 0:n], lhsT=slot[:, c, dd * 128:(dd + 1) * 128], rhs=oT[:, c, 0:n],
                                                   start=(c == 0), stop=(c == NH - 1))
                                return ins
                            T.op("pe", mm, reads=[stok] + otoks, writes=[("ps", b)])
                            T.op("dve", lambda e, d=d, b=b: e.tensor_tensor(out=hb[:, d, 0:n], in0=self.ps[b][:, 0:n], in1=hb[:, d, 0:n], op=ALU.add),
                                 reads=[("ps", b), (htok, d)], writes=[(htok, d)])
                        self.store_tile_group(dst, s, ti, a, n, hb, htok, dtag, ob_)
                        self.w_done()
                T.barrier()

    def pool_layer(self, s, j, src, dst, stag, dtag):
        T = self.T
        W = PTM + 2 * HALO
        U0 = HALO
        self.lazy_casts(s, "pool", j)
        with ExitStack() as es:
            hbs = [self.sb(es, [128, DC, W], F32, "hTp") for _ in range(2)]
            sq = self.sb(es, [128, DC, W], BF16, "sqp")
            xf = self.sb(es, [128, DC, W], F32, "xf")
            rstds = [self.sb(es, [128, W], F32, "rstdp") for _ in range(2)]
            A = self.sb(es, [128, 16, W], F32, "pA")
            B = self.sb(es, [128, 12, W], F32, "pB")
            ic = self.sb(es, [128, 4, 2 * HALO], F32, "ic")
            E = self.sb(es, [128, DC, HALO], F32, "pE")
            mx = self.sb(es, [128, DC, PTM], BF16, "mx")

            def geom(ti):
                a, n = PTILES[ti]
                lo = max(0, a - HALO)
                hi = min(L, a + n + HALO)
                return a, n, lo, hi - lo, lo - (a - HALO)

            def prep(ti):
                a, n, lo, nl, c0 = geom(ti)
                p = ti % 2
                hb, htok = hbs[p], "hT%d" % p
                self.load_tile(src, s, lo, nl, hb, htok, stag, col0=c0)
                self.rmsnorm_a(hb, htok, nl, sq, col0=c0)
                b = self.nb()
                ps = self.ps[b]

                def mm(e):
                    for c in range(DC):
                        ins = e.matmul(ps[:, 0:nl], lhsT=self.onesD[:, :], rhs=sq[:, c, 0:nl], start=(c == 0), stop=(c == DC - 1))
                    return ins
                T.op("pe", mm, reads=["sq", "onesD"], writes=[("ps", b)])
                rstd, rtok = rstds[p], "rstd%d" % p
                T.op("act", lambda e: e.activation(out=rstd[:, 0:nl], in_=ps[:, 0:nl], func=AF.Ln, bias=self.epst[:, 0:1], scale=1.0),
                     reads=[("ps", b), "eps"], writes=[rtok])
                T.op("act", lambda e: e.activation(out=rstd[:, 0:nl], in_=rstd[:, 0:nl], func=AF.Exp, scale=-0.5), reads=[rtok], writes=[rtok])

            prep(0)
            for ti in range(len(PTILES)):
                a, n, lo, nl, c0 = geom(ti)
                p = ti % 2
                hb, htok = hbs[p], "hT%d" % p
                rstd, rtok = rstds[p], "rstd%d" % p
                xtok = [("xf", c) for c in range(DC)]
                if c0 > 0:
                    T.op("dve", lambda e: e.memset(xf[:, :, 0:c0], 0.0), writes=xtok)
                if c0 + nl < n + 2 * HALO:
                    T.op("dve", lambda e: e.memset(xf[:, :, c0 + nl:n + 2 * HALO], 0.0), writes=xtok)
                gcol = V_POOLN + 16 * j
                for c in range(DC):
                    T.op("dve", lambda e, c=c: e.scalar_tensor_tensor(
                        out=xf[:, c, c0:c0 + nl], in0=hb[:, c, c0:c0 + nl], scalar=self.vecs[:, gcol + c:gcol + c + 1],
                        in1=rstd[:, 0:nl], op0=ALU.mult, op1=ALU.mult),
                        reads=[(htok, c), rtok, "vecs"], writes=[("xf", c)])
                if ti + 1 < len(PTILES):
                    prep(ti + 1)
                T.op("dve", lambda e: e.tensor_tensor(out=A[:, :, 1:W], in0=xf[:, :, 0:W - 1], in1=xf[:, :, 1:W], op=ALU.add),
                     reads=xtok, writes=["pA"])
                T.op("dve", lambda e: e.tensor_tensor(out=B[:, :, 2:W - 1], in0=A[:, 4:16, 1:W - 2], in1=A[:, 4:16, 3:W], op=ALU.add),
                     reads=["pA"], writes=["pB"])
                T.op("dve", lambda e: e.tensor_tensor(out=A[:, 8:16, 4:W - 3], in0=B[:, 4:12, 2:W - 5], in1=B[:, 4:12, 6:W - 1], op=ALU.add),
                     reads=["pB"], writes=["pA"])
                T.op("dve", lambda e: e.tensor_tensor(out=B[:, 8:12, 8:W - 7], in0=A[:, 12:16, 4:W - 11], in1=A[:, 12:16, 12:W - 3], op=ALU.add),
                     reads=["pA"], writes=["pB"])
                assert U0 + n <= W - 7
                srcs = [A[:, 0:4, :], B[:, 0:4, :], A[:, 8:12, :], B[:, 8:12, :]]
                stoks = ["pA", "pB", "pA", "pB"]
                for g, w in enumerate(POOL_W):
                    T.op("dve", lambda e, g=g, w=w: e.scalar_tensor_tensor(
                        out=mx[:, 4 * g:4 * g + 4, 0:n], in0=srcs[g][:, :, U0:U0 + n], scalar=1.0 / w,
                        in1=xf[:, 4 * g:4 * g + 4, U0:U0 + n], op0=ALU.mult, op1=ALU.subtract),
                        reads=[stoks[g]] + [("xf", 4 * g + c) for c in range(4)], writes=[("mx", 4 * g + c) for c in range(4)])
                edges = []
                if a == 0:
                    edges.append((0, 0))
                if a + n == L:
                    edges.append((n - HALO, 1))
                for (e0, ei) in edges:
                    T.dma("sp", out=ic[:, :, ei * HALO:(ei + 1) * HALO],
                          in_=self.icnt_d[:, :, a + e0:a + e0 + HALO].rearrange("g p t -> p g t"), writes=[("ic", ei)])
                    for c in range(DC):
                        g = c // 4
                        T.op("dve", lambda e, c=c, g=g, e0=e0, ei=ei: e.tensor_tensor(
                            out=E[:, c, :], in0=srcs[g][:, c % 4, U0 + e0:U0 + e0 + HALO], in1=ic[:, g, ei * HALO:(ei + 1) * HALO], op=ALU.mult),
                            reads=[stoks[g], ("ic", ei)], writes=[("pE", c)])
                    for c in range(DC):
                        T.op("dve", lambda e, c=c, e0=e0: e.tensor_tensor(
                            out=mx[:, c, e0:e0 + HALO], in0=E[:, c, :], in1=xf[:, c, U0 + e0:U0 + e0 + HALO], op=ALU.subtract),
                            reads=[("pE", c), ("xf", c)], writes=[("mx", c)])
                for g in range(4):
                    slot, stok = self.w_get(("pool", j, g))
                    for oc in range(4):
                        d = 4 * g + oc
                        b = self.nb()

                        def mm(e, oc=oc, b=b, slot=slot, g=g):
                            for icc in range(4):
                                ins = e.matmul(self.ps[b][:, 0:n], lhsT=slot[:, icc, oc * 128:(oc + 1) * 128], rhs=mx[:, 4 * g + icc, 0:n],
                                               start=(icc == 0), stop=(icc == 3))
                            return ins
                        T.op("pe", mm, reads=[stok] + [("mx", 4 * g + icc) for icc in range(4)], writes=[("ps", b)])
                        T.op("dve", lambda e, d=d, b=b: e.scalar_tensor_tensor(
                            out=hb[:, d, HALO:HALO + n], in0=self.ps[b][:, 0:n], scalar=self.vecs[:, V_POOLS + 16 * j + d:V_POOLS + 16 * j + d + 1],
                            in1=hb[:, d, HALO:HALO + n], op0=ALU.mult, op1=ALU.add),
                            reads=[("ps", b), (htok, d), "vecs"], writes=[(htok, d)])
                    self.w_done()
                self.store_tile(dst, s, ti, a, n, hb, htok, dtag, col0=HALO, tiles=PTILES)
            T.barrier()


def _rope_tables():
    t = np.arange(SEQ)
    r = (t // GRID_W).astype(np.float32)
    c = (t % GRID_W).astype(np.float32)
    axis_dim = HD // 2
    inv_freq = (10000.0 ** (-np.arange(0, axis_dim, 2, dtype=np.float32) / axis_dim)).astype(np.float32)
    ang = np.concatenate([r[:, None] * inv_freq[None], c[:, None] * inv_freq[None]], axis=-1)
    ang = np.concatenate([np.zeros((NMETA, HD // 2), np.float32), ang], axis=0)
    cos = np.cos(ang).astype(np.float32).T
    sin = np.sin(ang).astype(np.float32).T
    cosT = np.concatenate([cos, cos], axis=0)
    sinS = np.concatenate([-sin, sin], axis=0)
    return np.ascontiguousarray(cosT), np.ascontiguousarray(sinS)


def _inv_counts():
    t = np.arange(L)
    out = np.zeros((4, 128, L), np.float32)
    for g, w in enumerate(POOL_W):
        lo = np.clip(t - w // 2, 0, L)
        hi = np.clip(t - w // 2 + w, 0, L)
        cnt = (hi - lo).astype(np.float32)
        out[g] = (np.float32(1.0) / cnt)[None, :]
    return out


_PERM = np.concatenate([np.arange(0, HD, 2), np.arange(1, HD, 2)])


def _chunk_cols(v):
    return np.ascontiguousarray(np.asarray(v, np.float32).reshape(DC, 128).T)


def _prep_shared(inp):
    w_qkv = np.asarray(inp["w_qkv"], np.float32)
    nq = NH * HD
    colperm = np.arange(3072)
    for h in range(NH):
        colperm[h * HD:(h + 1) * HD] = h * HD + _PERM
    for h in range(NKV):
        colperm[nq + h * HD:nq + (h + 1) * HD] = nq + h * HD + _PERM
    w_qkv_p = np.ascontiguousarray(w_qkv[:, :, colperm])
    vecs = np.zeros((128, NV), np.float32)
    for j in range(2):
        vecs[:, V_ATTN + 16 * j:V_ATTN + 16 * j + 16] = _chunk_cols(inp["attn_norm"][j])
        vecs[:, V_POOLN + 16 * j:V_POOLN + 16 * j + 16] = _chunk_cols(inp["pool_norm"][j])
        vecs[:, V_POOLS + 16 * j:V_POOLS + 16 * j + 16] = _chunk_cols(inp["pool_scale"][j])
        vecs[:, V_QG + j] = np.asarray(inp["q_norm"], np.float32)[j][_PERM]
        vecs[:, V_KG + j] = np.asarray(inp["k_norm"], np.float32)[j][_PERM]
    for i in range(4):
        vecs[:, V_MLP + 16 * i:V_MLP + 16 * i + 16] = _chunk_cols(inp["mlp_norm"][i])
    vecs[:, V_FINAL:V_FINAL + 16] = _chunk_cols(inp["final_norm"])
    grep = np.zeros((4, 128, 128), np.float32)
    for j in range(2):
        grep[j] = np.asarray(inp["q_norm"], np.float32)[j][None, :]
        grep[2 + j] = np.asarray(inp["k_norm"], np.float32)[j][None, :]
    rot = np.zeros((128, 128), np.float32)
    for m in range(128):
        rot[(m + 64) % 128, m] = 1.0
    cosT, sinS = _rope_tables()
    return {
        "w_qkv": w_qkv_p,
        "w_o": np.ascontiguousarray(np.asarray(inp["w_o"], np.float32)),
        "w_pool": np.ascontiguousarray(np.asarray(inp["w_pool"], np.float32).reshape(2, 2048, 512)),
        "w_up": np.ascontiguousarray(np.asarray(inp["w_up"], np.float32)),
        "w_down": np.ascontiguousarray(np.asarray(inp["w_down"], np.float32)),
        "vecs": vecs, "grep": grep, "rot": rot, "cosT": cosT, "sinS": sinS, "icnt": _inv_counts(),
    }


def _prep_h0(xs, meta):
    out = np.empty((len(xs), D, L), np.float32)
    mT = np.asarray(meta, np.float32).T
    for i, x in enumerate(xs):
        out[i, :, :NMETA] = mT
        out[i, :, NMETA:] = np.asarray(x, np.float32).T
    return out


_CACHE = {}


def _get_prog(nseq, nlayers):
    key = (nseq, nlayers)
    if key not in _CACHE:
        p = Prog(nseq, nlayers)
        _CACHE[key] = p.build()
    return _CACHE[key]


def run_sequences(inp, seqs, ncores, nseq, nlayers=4, trace=False):
    shared = _prep_shared(inp)
    nc = _get_prog(nseq, nlayers)
    in_maps = []
    for c in range(ncores):
        m = dict(shared)
        m["h0"] = _prep_h0(seqs[c * nseq:(c + 1) * nseq], inp["meta_tokens"])
        in_maps.append(m)
    res = run_bass_kernel_spmd(nc, in_maps, core_ids=list(range(ncores)), trace=trace)
    outs = []
    for c in range(ncores):
        yT = res.results[c]["yT"]
        for k in range(nseq):
            outs.append(np.ascontiguousarray(yT[k].T))
    return outs, res


def kernel(**inputs):
    xp = np.asarray(inputs["x_prompt"], np.float32)
    xs = np.asarray(inputs["x_sample"], np.float32)
    seqs = [xp[b] for b in range(xp.shape[0])] + [xs[b] for b in range(xs.shape[0])]
    nseq = len(seqs) // NCORES
    outs, _ = run_sequences(inputs, seqs, NCORES, nseq)
    y_prompt = np.stack(outs[:xp.shape[0]], axis=0)
    y_sample = np.stack(outs[xp.shape[0]:], axis=0)
    return (y_prompt, y_sample)
```
